# Optimizing a Trainium2 kernel written in Bass

```python
import jax, jax.numpy as jnp
from jax import lax
import numpy as np

D_MODEL = 2048
BATCH = 8
SEQ = 2048
DEPTH = 1

CHUNK = 64
PLE_DIM = 256
DN_HEADS = 16
DN_HEAD_DIM = 128
DN_WIDTH = DN_HEADS * DN_HEAD_DIM
SHORT_CONV = 4
CONV_CH = D_MODEL
CONV_KERNEL = 31
N_BRANCHES = 2
EPS = 1e-6
IN_SIZES = (3 * DN_WIDTH,
            DN_WIDTH,
            DN_HEADS,
            DN_HEADS,
            2 * CONV_CH,
            CONV_CH,
            N_BRANCHES * D_MODEL)
IN_COLS = sum(IN_SIZES)

kernel_name = "hybrid_gdn_conformer_streaming_block"


def _split_points():
    return np.cumsum(np.array(IN_SIZES))[:-1].tolist()


def rmsnorm(x, g):
    xf = x.astype(jnp.float32)
    y = xf * lax.rsqrt(jnp.mean(xf * xf, axis=-1, keepdims=True) + EPS)
    return (y * g.astype(jnp.float32)).astype(x.dtype)


def layernorm(x, g, b):
    xf = x.astype(jnp.float32)
    mu = jnp.mean(xf, axis=-1, keepdims=True)
    var = jnp.mean(jnp.square(xf - mu), axis=-1, keepdims=True)
    y = (xf - mu) * lax.rsqrt(var + EPS)
    return (y * g.astype(jnp.float32) + b.astype(jnp.float32)).astype(x.dtype)


def l2norm(x):
    return x * lax.rsqrt(jnp.sum(x * x, axis=-1, keepdims=True) + EPS)


def causal_depthwise_conv(x, w):
    K = w.shape[0]
    xp = jnp.pad(x, ((0, 0), (K - 1, 0), (0, 0)))
    return lax.conv_general_dilated(
        xp, w[:, None, :].astype(x.dtype), window_strides=(1,), padding='VALID',
        dimension_numbers=('NWC', 'WIO', 'NWC'), feature_group_count=x.shape[-1])


def gated_delta_chunked(q, k, v, g, beta):
    B, S, H, Dk = q.shape
    nc = S // CHUNK

    def to_chunks(t):
        t = t.reshape((B, nc, CHUNK, H) + t.shape[3:])
        return jnp.moveaxis(t, 3, 1)

    q, k, v = to_chunks(q) * (Dk ** -0.5), to_chunks(k), to_chunks(v)
    g, beta = to_chunks(g), to_chunks(beta)
    gc = jnp.cumsum(g, axis=-1)
    idx = jnp.arange(CHUNK)
    causal = idx[:, None] >= idx[None, :]
    strict = idx[:, None] > idx[None, :]
    decay = jnp.exp(jnp.where(causal, gc[..., :, None] - gc[..., None, :], -jnp.inf))
    kb = k * beta[..., None]
    A = jnp.einsum('bhnid,bhnjd->bhnij', kb, k) * decay * strict
    eye = jnp.eye(CHUNK, dtype=jnp.float32)
    T = lax.linalg.triangular_solve(eye + A, jnp.broadcast_to(eye, A.shape),
                                    left_side=True, lower=True, unit_diagonal=True)
    u = jnp.einsum('bhnij,bhnjd->bhnid', T, v * beta[..., None])
    w = jnp.einsum('bhnij,bhnjd->bhnid', T, kb * jnp.exp(gc)[..., None])
    qk = jnp.einsum('bhnid,bhnjd->bhnij', q, k) * decay
    q_dec = q * jnp.exp(gc)[..., None]
    g_last = gc[..., -1]
    k_end = k * jnp.exp(g_last[..., None] - gc)[..., None]
    xs = tuple(jnp.moveaxis(t, 2, 0) for t in (u, w, qk, q_dec, k_end, g_last))

    def step(Sst, inp):
        u_n, w_n, qk_n, qd_n, ke_n, gl_n = inp
        v_new = u_n - jnp.einsum('bhid,bhde->bhie', w_n, Sst)
        o = jnp.einsum('bhid,bhde->bhie', qd_n, Sst) + jnp.einsum('bhij,bhje->bhie', qk_n, v_new)
        Sst = Sst * jnp.exp(gl_n)[..., None, None] + jnp.einsum('bhid,bhie->bhde', ke_n, v_new)
        return Sst, o

    S0 = jnp.zeros((B, H, Dk, v.shape[-1]), jnp.float32)
    _, o = lax.scan(step, S0, xs)
    return jnp.transpose(o, (1, 0, 3, 2, 4)).reshape(B, S, H, v.shape[-1])


def setup_inputs(seed: int = 0) -> dict:
    key = jax.random.key(seed)
    ks = jax.random.split(key, 24)
    f32 = jnp.float32
    nrm = lambda k, shape, s: jax.random.normal(k, shape, f32) * s
    gain = lambda k, shape: 1.0 + 0.05 * jax.random.normal(k, shape, f32)
    dt = jnp.exp(jax.random.uniform(ks[7], (DEPTH, DN_HEADS), f32, np.log(1e-3), np.log(1e-1)))
    return {
        "x": nrm(ks[0], (BATCH, SEQ, D_MODEL), 1.0),
        "p": nrm(ks[1], (DEPTH, BATCH, SEQ, PLE_DIM), 1.0),
        "g_pre": gain(ks[2], (DEPTH, D_MODEL)),
        "w_in": nrm(ks[3], (DEPTH, D_MODEL, IN_COLS), D_MODEL ** -0.5),
        "b_gate": nrm(ks[4], (DEPTH, N_BRANCHES * D_MODEL), 0.02),
        "w_conv_qkv": nrm(ks[5], (DEPTH, SHORT_CONV, 3 * DN_WIDTH), SHORT_CONV ** -0.5),
        "a_log": jnp.log(jax.random.uniform(ks[6], (DEPTH, DN_HEADS), f32, 1.0, 16.0)),
        "dt_bias": dt + jnp.log(-jnp.expm1(-dt)),
        "g_dn_out": gain(ks[8], (DEPTH, DN_HEAD_DIM)),
        "w_dw": nrm(ks[9], (DEPTH, CONV_KERNEL, CONV_CH), CONV_KERNEL ** -0.5),
        "b_dw": nrm(ks[10], (DEPTH, CONV_CH), 0.02),
        "ln_g": gain(ks[11], (DEPTH, CONV_CH)),
        "ln_b": nrm(ks[12], (DEPTH, CONV_CH), 0.02),
        "w_br_a": nrm(ks[13], (DEPTH, DN_WIDTH, D_MODEL), DN_WIDTH ** -0.5),
        "w_br_b": nrm(ks[14], (DEPTH, CONV_CH, D_MODEL), CONV_CH ** -0.5),
        "w_out": nrm(ks[15], (DEPTH, D_MODEL, D_MODEL), D_MODEL ** -0.5),
        "g_post": gain(ks[16], (DEPTH, D_MODEL)),
        "w_ple_gate": nrm(ks[17], (DEPTH, D_MODEL, D_MODEL), D_MODEL ** -0.5),
        "w_ple_proj": nrm(ks[18], (DEPTH, PLE_DIM, D_MODEL), PLE_DIM ** -0.5),
        "g_ple": gain(ks[19], (DEPTH, D_MODEL)),
    }


def reference(x, p, g_pre, w_in, b_gate, w_conv_qkv, a_log, dt_bias, g_dn_out, w_dw, b_dw,
              ln_g, ln_b, w_br_a, w_br_b, w_out, g_post, w_ple_gate, w_ple_proj, g_ple):
    B, S, _ = x.shape
    f32 = jnp.float32
    for i in range(DEPTH):
        h = rmsnorm(x, g_pre[i])
        proj = h @ w_in[i]
        qkv, z_a, b_raw, a_raw, glu_in, z_b, gate_logits = jnp.split(proj, _split_points(), axis=-1)

        qkv = jax.nn.silu(causal_depthwise_conv(qkv, w_conv_qkv[i]))
        q, k, v = jnp.split(qkv, 3, axis=-1)
        heads = lambda t: t.reshape(B, S, DN_HEADS, DN_HEAD_DIM).astype(f32)
        q, k, v = l2norm(heads(q)), l2norm(heads(k)), heads(v)
        beta = jax.nn.sigmoid(b_raw.astype(f32))
        g_log = -jnp.exp(a_log[i].astype(f32)) * jax.nn.softplus(a_raw.astype(f32) + dt_bias[i].astype(f32))
        o_a = gated_delta_chunked(q, k, v, g_log, beta)
        o_a = rmsnorm(o_a, g_dn_out[i]).reshape(B, S, DN_WIDTH).astype(x.dtype) * jax.nn.silu(z_a)
        y_a = o_a @ w_br_a[i]

        u = jax.nn.glu(glu_in, axis=-1)
        u = causal_depthwise_conv(u, w_dw[i]) + b_dw[i]
        u = jax.nn.silu(layernorm(u, ln_g[i], ln_b[i])) * jax.nn.silu(z_b)
        y_b = u @ w_br_b[i]

        gate_a, gate_b = jnp.split(jax.nn.sigmoid(gate_logits + b_gate[i]), N_BRANCHES, axis=-1)
        mixed = (gate_a * y_a + gate_b * y_b) @ w_out[i]
        x = x + rmsnorm(mixed, g_post[i])

        e = p[i] @ w_ple_proj[i]
        x = x + rmsnorm(jax.nn.sigmoid(x @ w_ple_gate[i]) * e, g_ple[i])
    return x
```

```python
import math
from contextlib import ExitStack

import numpy as np
import concourse.bass as bass
import concourse.mybir as mybir
from concourse.bass_utils import run_bass_kernel_spmd

F32 = mybir.dt.float32
BF = mybir.dt.bfloat16
AF = mybir.ActivationFunctionType
ALU = mybir.AluOpType
AX = mybir.AxisListType

D = 2048
KC = 16
H = 16
PLE = 256
INC = 18464
OFF_Q, OFF_K, OFF_V, OFF_ZA, OFF_BG, OFF_GLU, OFF_ZB, OFF_GATE = 0, 2048, 4096, 6144, 8192, 8224, 12320, 14368
EPS = 1e-6
CK = 31

C_GPRE = 0
C_BGATE = 16
C_WCONV = 48
C_WDW = 240
C_BDW = 736
C_LNG = 752
C_LNB = 768
C_GDN = 784
C_ALOG = 785
C_DTB = 801
NS = 820
NEG = -30000.0
NCST = 20
KSTOP = 9
MMG = 16
KPAIRS = 99
KSTEPS = 999
KOFF = -1
KWIN = 1


class Buf:
    __slots__ = ("w", "r", "name")

    def __init__(self, name=""):
        self.w = None
        self.r = {}
        self.name = name


class Rec:
    __slots__ = ("waits_c", "waits_d", "fn", "dma_sem", "needed", "val")

    def __init__(self, waits_c, waits_d, fn, dma_sem):
        self.waits_c = waits_c
        self.waits_d = waits_d
        self.fn = fn
        self.dma_sem = dma_sem
        self.needed = False
        self.val = 0


class EngState:
    def __init__(self, name):
        self.name = name
        self.ops = []
        self.seen_c = {}
        self.seen_d = {}
        self.last_c = -1


class Prog:
    ENGS = ("pe", "act", "dve", "pool", "sp")

    def __init__(self):
        self.E = {n: EngState(n) for n in self.ENGS}
        self.dcount = {}

    def op(self, eng, fn, reads=(), writes=(), dma_sem=None):
        e = self.E[eng]
        need_c = {}
        need_d = {}

        def add(tok, raw):
            if tok is None:
                return
            if tok[0] == "c":
                en, idx = tok[1], tok[2]
                if en == eng and dma_sem is None:
                    if eng == "pe":
                        return
                if e.seen_c.get(en, -1) >= idx:
                    return
                if need_c.get(en, -1) < idx:
                    need_c[en] = idx
            else:
                sk, val = tok[1], tok[2]
                if e.seen_d.get(sk, 0) >= val:
                    return
                if need_d.get(sk, 0) < val:
                    need_d[sk] = val

        for b in reads:
            add(b.w, True)
        for b in writes:
            add(b.w, False)
            for t in b.r.values():
                add(t, False)
        for en, idx in need_c.items():
            e.seen_c[en] = idx
            self.E[en].ops[idx].needed = True
        for sk, val in need_d.items():
            e.seen_d[sk] = val
        rec = Rec(need_c, need_d, fn, dma_sem)
        e.ops.append(rec)
        idx = len(e.ops) - 1
        if dma_sem is None:
            tok = ("c", eng, idx)
            key = eng
            e.last_c = idx
        else:
            self.dcount[dma_sem] = self.dcount.get(dma_sem, 0) + 16
            tok = ("d", dma_sem, self.dcount[dma_sem])
            key = ("d", dma_sem)
        for b in reads:
            b.r[key] = tok
        for b in writes:
            b.w = tok
            b.r = {}

    def barrier(self):
        for eng in self.ENGS:
            e = self.E[eng]
            need_c = {}
            need_d = {}
            for en in self.ENGS:
                o = self.E[en]
                if en == eng or o.last_c < 0:
                    continue
                if e.seen_c.get(en, -1) < o.last_c:
                    need_c[en] = o.last_c
                    e.seen_c[en] = o.last_c
                    o.ops[o.last_c].needed = True
            for sk, val in self.dcount.items():
                if e.seen_d.get(sk, 0) < val:
                    need_d[sk] = val
                    e.seen_d[sk] = val
            if need_c or need_d:
                e.ops.append(Rec(need_c, need_d, None, None))

    def finalize(self):
        for e in self.E.values():
            cum = 0
            for rec in e.ops:
                if rec.fn is not None and rec.dma_sem is None and rec.needed:
                    cum += 1
                    rec.val = cum

    def replay(self, eng, h, esem, dsem):
        e = self.E[eng]
        for rec in e.ops:
            for en, idx in rec.waits_c.items():
                h.wait_ge(esem[en], self.E[en].ops[idx].val)
            for sk, val in rec.waits_d.items():
                h.wait_ge(dsem[sk], val)
            if rec.fn is None:
                continue
            ins = rec.fn(h)
            if rec.dma_sem is not None:
                ins.then_inc(dsem[rec.dma_sem], 16)
            elif rec.needed:
                ins.then_inc(esem[eng], 1)


class T:
    __slots__ = ("ap", "b")

    def __init__(self, ap, name=""):
        self.ap = ap
        self.b = Buf(name)


def build_nc(S, TB, dbg=False):
    NB = TB // 128
    NBLK = S // TB
    assert TB % 128 == 0 and TB <= 512 and S % TB == 0
    nc = bass.Bass("TRN2", target_bir_lowering=False)
    x_d = nc.dram_tensor("x", [S, D], F32, kind="ExternalInput").ap()
    p_d = nc.dram_tensor("p", [S, PLE], F32, kind="ExternalInput").ap()
    w_in_d = nc.dram_tensor("w_in", [D, INC], F32, kind="ExternalInput").ap()
    w_bra_d = nc.dram_tensor("w_br_a", [D, D], F32, kind="ExternalInput").ap()
    w_brb_d = nc.dram_tensor("w_br_b", [D, D], F32, kind="ExternalInput").ap()
    w_out_d = nc.dram_tensor("w_out", [D, D], F32, kind="ExternalInput").ap()
    w_pg_d = nc.dram_tensor("w_ple_gate", [D, D], F32, kind="ExternalInput").ap()
    w_pp_d = nc.dram_tensor("w_ple_proj", [PLE, D], F32, kind="ExternalInput").ap()
    sm_d = nc.dram_tensor("smalls", [128, NS], F32, kind="ExternalInput").ap()
    rows_d = nc.dram_tensor("rows", [128, 2 * D], F32, kind="ExternalInput").ap()
    cst_d = nc.dram_tensor("cst", [128, NCST * 128], F32, kind="ExternalInput").ap()
    out_d = nc.dram_tensor("out", [S, D], F32, kind="ExternalOutput").ap()
    dbg_d = None
    if dbg:
        dbg_d = nc.dram_tensor("dbg", [128, 4, 16, TB], F32, kind="ExternalOutput").ap()

    P = Prog()
    NW = 8
    ARENA_F32 = 51200

    es = ExitStack()
    arena = es.enter_context(nc.sbuf_tensor("arena", [128, ARENA_F32], F32))
    psum = es.enter_context(nc.psum_tensor("psum", [128, 8, 512], F32))
    PS = [T(psum[:, i, :], f"ps{i}") for i in range(8)]

    class Alloc:
        def __init__(self, base, limit):
            self.base = base
            self.off = base
            self.limit = limit

        def reset(self):
            self.off = self.base

        def get(self, shape, dt, name=""):
            n = 1
            for s in shape[1:]:
                n *= s
            nbytes = n * (4 if dt == F32 else 2)
            nw = (nbytes + 3) // 4
            assert self.off + nw <= self.limit, (name, self.off, nw, self.limit)
            ap = arena[:, self.off:self.off + nw]
            self.off += nw
            if dt != F32:
                ap = ap.bitcast(dt)
                if nbytes % 4:
                    ap = ap[:, 0:n]
            if len(shape) == 3:
                ap = ap.rearrange("p (a b) -> p a b", a=shape[1])
            elif len(shape) == 4:
                ap = ap.rearrange("p (a b c) -> p a b c", a=shape[1], b=shape[2])
            return T(ap, name)

    pa = Alloc(0, ARENA_F32)
    cb = pa.get([128, NCST, 128], BF, "cb")
    identb, uleb, ugtb, onesb = cb.ap[:, 0, :], cb.ap[:, 1, :], cb.ap[:, 2, :], cb.ap[:, 3, :]
    negm4 = pa.get([128, NB, 128], BF, "negm4")
    strict4 = pa.get([128, NB, 128], F32, "strict4")
    sm = pa.get([128, NS], F32, "sm")
    negA = pa.get([128, 16], F32, "negA")
    tails_qkv = pa.get([128, 48, 3], F32, "tails_qkv")
    tails_u = pa.get([128, 16, 30], BF, "tails_u")
    Sst = pa.get([128, 16, 128], F32, "Sst")
    Sst_h = [Buf(f"S{h}") for h in range(H)]
    wpool = pa.get([128, NW + 1, 16, 128], BF, "wpool")
    wslot = [Buf(f"w{i}") for i in range(NW + 1)]
    arX0 = pa.off
    hT = pa.get([128, KC, TB], BF, "hT")
    oaT = pa.get([128, KC, TB], BF, "oaT")
    arX1 = pa.off
    arY0 = pa.off
    ubT = pa.get([128, KC, TB], BF, "ubT")
    arZ0 = pa.off
    minT = pa.get([128, KC, TB], BF, "minT")
    alX = Alloc(arX0, arX1)
    mixed = alX.get([128, NB, D], F32, "mixed")
    alY = Alloc(arY0, arZ0)
    x1T = alY.get([128, KC, TB], BF, "x1T")
    wk = Alloc(pa.off, ARENA_F32)

    wptr = [0]
    bigp = [0]
    smallp = [0]

    def big():
        b = PS[bigp[0] % 3]
        bigp[0] += 1
        return b

    def small():
        b = PS[3 + smallp[0] % 5]
        smallp[0] += 1
        return b

    def mm(out, lhsT, rhs, start, stop, R, W):
        P.op("pe", lambda t: t.matmul(out, lhsT=lhsT, rhs=rhs, start=start, stop=stop), R, W)

    def act(out, in_, func, R, W, **kw):
        P.op("act", lambda s: s.activation(out=out, in_=in_, func=func, **kw), R, W)

    def tt(out, in0, in1, op, R, W, eng="dve"):
        P.op(eng, lambda v: v.tensor_tensor(out=out, in0=in0, in1=in1, op=op), R, W)

    def ts(out, in0, s1, op0, R, W, s2=None, op1=None, eng="dve"):
        if op1 is None:
            P.op(eng, lambda v: v.tensor_scalar(out=out, in0=in0, scalar1=s1, scalar2=None, op0=op0), R, W)
        else:
            P.op(eng, lambda v: v.tensor_scalar(out=out, in0=in0, scalar1=s1, scalar2=s2, op0=op0, op1=op1), R, W)

    def stt(out, in0, scalar, in1, op0, op1, R, W):
        P.op("dve", lambda v: v.scalar_tensor_tensor(out=out, in0=in0, scalar=scalar, in1=in1, op0=op0, op1=op1), R, W)

    def cp(out, in_, R, W, eng="dve"):
        if eng == "act":
            P.op("act", lambda s: s.activation(out=out, in_=in_, func=AF.Copy), R, W)
        else:
            P.op(eng, lambda v: v.tensor_copy(out=out, in_=in_), R, W)

    def memset(ap, val, W):
        P.op("dve", lambda v: v.memset(ap, val), (), W)

    def dma(eng, out, in_, R, W, sem):
        P.op(eng, lambda q: q.dma_start(out=out, in_=in_), R, W, dma_sem=sem)

    def load_w(src, ncols=128, k=D):
        s = wptr[0] % NW
        wptr[0] += 1
        nk = k // 128
        dma("pool", wpool.ap[:, s, 0:nk, 0:ncols], src.rearrange("(kc p) c -> p kc c", p=128), (), (wslot[s],), f"w{s}")
        return s

    def load_wide(src512):
        while wptr[0] % 4:
            wptr[0] += 1
        s0 = wptr[0] % NW
        for j in range(4):
            load_w(src512[:, j * 128:(j + 1) * 128])
        return s0

    def rsqrt_small(out, in_, R, W, tmp, scale=1.0, post_bias=0.0):
        act(tmp.ap, in_, AF.Ln, R, (tmp.b,), scale=scale, bias=EPS)
        act(out, tmp.ap, AF.Exp, (tmp.b,), W, scale=-0.5, bias=post_bias)

    def bc(ap2, n):
        return ap2.unsqueeze(2).to_broadcast([128, ap2.shape[1], n])

    def bcm(ap2, a):
        return ap2.unsqueeze(1).to_broadcast([128, a, ap2.shape[1]])

    wk.reset()
    cstf = wk.get([128, NCST, 128], F32, "cstf")
    dma("sp", cstf.ap, cst_d.rearrange("p (a b) -> p a b", a=NCST), (), (cstf.b,), "cst")
    dma("sp", sm.ap, sm_d, (), (sm.b,), "sm")
    cp(cb.ap, cstf.ap, (cstf.b,), (cb.b,))
    cp(negm4.ap, bcm(cstf.ap[:, 4, :], NB), (cstf.b,), (negm4.b,))
    cp(strict4.ap, bcm(cstf.ap[:, 5, :], NB), (cstf.b,), (strict4.b,))
    act(negA.ap, sm.ap[:, C_ALOG:C_ALOG + 16], AF.Exp, (sm.b,), (negA.b,))
    ts(negA.ap, negA.ap, -1.0, ALU.mult, (negA.b,), (negA.b,))
    memset(tails_qkv.ap, 0.0, (tails_qkv.b,))
    memset(tails_u.ap, 0.0, (tails_u.b,))
    memset(Sst.ap, 0.0, [Sst.b] + Sst_h)
    P.barrier()

    def smc(c):
        return sm.ap[:, c:c + 1]

    for blk in range(NBLK):
        t0 = blk * TB
        wk.reset()
        xt = [wk.get([128, D], F32, f"xt{i}") for i in range(2)]
        xjunk = wk.get([128, D], BF, "xjunk")
        xn = wk.get([128, D], BF, "xn")
        ss0 = wk.get([128, NB], F32, "ss0")
        l0 = wk.get([128, NB], F32, "l0")
        rstd0 = wk.get([128, NB], F32, "rstd0")
        hTk = [Buf(f"hT{n}") for n in range(NB)]
        for n in range(NB):
            xs = xt[n % 2]
            dma("sp", xs.ap, x_d[t0 + n * 128:t0 + (n + 1) * 128, :], (), (xs.b,), f"xt{n % 2}")
            act(xjunk.ap, xs.ap, AF.Square, (xs.b,), (xjunk.b, ss0.b), accum_out=ss0.ap[:, n:n + 1])
            act(l0.ap[:, n:n + 1], ss0.ap[:, n:n + 1], AF.Ln, (ss0.b,), (l0.b,), scale=1.0 / D, bias=EPS)
            act(rstd0.ap[:, n:n + 1], l0.ap[:, n:n + 1], AF.Exp, (l0.b,), (rstd0.b,), scale=-0.5)
            ts(xn.ap, xs.ap, rstd0.ap[:, n:n + 1], ALU.mult, (xs.b, rstd0.b), (xn.b,))
            for g in range(4):
                bk = big()
                for j in range(4):
                    kc = g * 4 + j
                    mm(bk.ap[:, j * 128:(j + 1) * 128], xn.ap[:, kc * 128:(kc + 1) * 128], identb, True, True,
                       (xn.b, cb.b), (bk.b,))
                tt(hT.ap[:, g * 4:g * 4 + 4, n * 128:(n + 1) * 128], bk.ap.rearrange("p (a b) -> p a b", a=4),
                   bc(sm.ap[:, C_GPRE + g * 4:C_GPRE + g * 4 + 4], 128), ALU.mult, (bk.b, sm.b), (hTk[n],))
        HT = tuple(hTk)

        if KSTOP < 1:
            continue
        P.barrier()
        wk.reset()
        bgs = wk.get([128, NB, 32], F32, "bgs")
        beta = wk.get([128, NB, 16], F32, "beta")
        t16 = wk.get([128, NB, 16], F32, "t16")
        gg = wk.get([128, NB, 16], F32, "gg")
        ghl = wk.get([128, NB, 32], BF, "ghl")
        egkl = wk.get([128, NB, 48], F32, "egkl")
        sw = load_w(w_in_d[:, OFF_BG:OFF_BG + 32], ncols=32)
        for n in range(NB):
            bk = small()
            for kc in range(KC):
                mm(bk.ap[:, 0:32], hT.ap[:, kc, n * 128:(n + 1) * 128], wpool.ap[:, sw, kc, 0:32], kc == 0, kc == KC - 1,
                   (hTk[n], wslot[sw]), (bk.b,))
            cp(bgs.ap[:, n, :], bk.ap[:, 0:32], (bk.b,), (bgs.b,), eng="act")
        act(beta.ap, bgs.ap[:, :, 0:16], AF.Sigmoid, (bgs.b,), (beta.b,))
        tt(t16.ap, bgs.ap[:, :, 16:32], bcm(sm.ap[:, C_DTB:C_DTB + 16], NB), ALU.add, (bgs.b, sm.b), (t16.b,))
        act(t16.ap, t16.ap, AF.Exp, (t16.b,), (t16.b,))
        act(t16.ap, t16.ap, AF.Ln, (t16.b,), (t16.b,), bias=1.0)
        tt(gg.ap, t16.ap, bcm(negA.ap, NB), ALU.mult, (t16.b, negA.b), (gg.b,))
        cp(ghl.ap[:, :, 0:16], gg.ap, (gg.b,), (ghl.b,))
        tt(ghl.ap[:, :, 16:32], gg.ap, ghl.ap[:, :, 0:16], ALU.subtract, (gg.b, ghl.b), (ghl.b,))
        for n in range(NB):
            bk = small()
            for q, lt in enumerate((uleb, ugtb, onesb)):
                mm(bk.ap[:, q * 16:(q + 1) * 16], lt, ghl.ap[:, n, 0:16], True, False, (cb.b, ghl.b), (bk.b,))
                mm(bk.ap[:, q * 16:(q + 1) * 16], lt, ghl.ap[:, n, 16:32], False, True, (cb.b, ghl.b), (bk.b,))
            act(egkl.ap[:, n, :], bk.ap[:, 0:48], AF.Exp, (bk.b,), (egkl.b,))

        pre = [wk.get([128, TB + 3], F32, f"pre{i}") for i in range(2)]
        cacc = [wk.get([128, TB], F32, f"cacc{i}") for i in range(2)]
        s1 = [[wk.get([128, TB], BF, f"s1_{par}_{i}") for i in range(4)] for par in range(4)]
        sqs = wk.get([128, NB, 128], F32, "sqs")
        ojunk = wk.get([128, 128], BF, "ojunk")

        def nsl(n):
            return slice(n * 128, (n + 1) * 128)

        def stage1(h):
            par = h % 4
            steps = []

            def mk(idx, off):
                st = {}
                tix = off // 128 + h

                def post():
                    bk = st["bk"]
                    if idx < 3:
                        pr = pre[idx % 2]
                        cp(pr.ap[:, 0:3], tails_qkv.ap[:, tix, :], (tails_qkv.b,), (pr.b,))
                        cp(pr.ap[:, 3:3 + TB], bk.ap[:, 0:TB], (bk.b,), (pr.b,), eng="act")
                        cp(tails_qkv.ap[:, tix, :], pr.ap[:, TB:TB + 3], (pr.b,), (tails_qkv.b,))
                        ac = cacc[idx % 2]
                        wc = C_WCONV + tix * 4
                        ts(ac.ap, pr.ap[:, 0:TB], smc(wc), ALU.mult, (pr.b, sm.b), (ac.b,))
                        for k in range(1, 4):
                            stt(ac.ap, pr.ap[:, k:k + TB], smc(wc + k), ac.ap, ALU.mult, ALU.add, (pr.b, sm.b, ac.b), (ac.b,))
                        act(s1[par][idx].ap, ac.ap, AF.Silu, (ac.b,), (s1[par][idx].b,))
                    else:
                        act(s1[par][3].ap, bk.ap[:, 0:TB], AF.Silu, (bk.b,), (s1[par][3].b,))

                def grp(k0, k1):
                    def f():
                        if k0 == 0:
                            st["s"] = pre_slots[h][idx]
                            st["bk"] = big()
                        s_, bk = st["s"], st["bk"]
                        for kc in range(k0, k1):
                            mm(bk.ap[:, 0:TB], wpool.ap[:, s_, kc, :], hT.ap[:, kc, :], kc == 0, kc == KC - 1,
                               HT + (wslot[s_],), (bk.b,))
                        if k1 == KC:
                            post()
                            s1_done[h] = s1_done.get(h, 0) + 1
                    return f

                return [grp(k, k + MMG) for k in range(0, KC, MMG)]

            for idx, off in enumerate((OFF_Q, OFF_K, OFF_V, OFF_ZA)):
                steps += mk(idx, off)
            return steps

        def make_env(tag):
            ks_tok = wk.get([128, NB, 128], BF, "ks_tok")
            qs_tok = wk.get([128, NB, 128], BF, "qs_tok")
            vb = wk.get([128, NB, 128], BF, "vb")
            ssk = wk.get([128, NB], F32, "ssk")
            ssq = wk.get([128, NB], F32, "ssq")
            ltmp = wk.get([128, NB], F32, "ltmp")
            rs_k = wk.get([128, NB], F32, "rs_k")
            rs_q = wk.get([128, NB], F32, "rs_q")
            c_t = wk.get([128, NB], F32, "c_t")
            c_kbg = wk.get([128, NB], F32, "c_kbg")
            c_ke = wk.get([128, NB], F32, "c_ke")
            c_qd = wk.get([128, NB], F32, "c_qd")
            Dk = wk.get([128, NB, 128], BF, "Dk")
            Dq = wk.get([128, NB, 128], BF, "Dq")
            Dqd = wk.get([128, NB, 128], BF, "Dqd")
            Do = Dk
            knT = wk.get([128, TB], BF, "knT")
            qnT = wk.get([128, TB], BF, "qnT")
            qdT = wk.get([128, TB], BF, "qdT")
            kbg = wk.get([128, NB, 128], BF, "kbg")
            ke = wk.get([128, NB, 128], BF, "ke")
            Rm = wk.get([128, NB, 128], BF, "Rm")
            SBm = wk.get([128, NB, 128], BF, "SBm")
            Mfull = wk.get([128, NB, 128], BF, "Mfull")
            Mb = wk.get([128, NB, 128], BF, "Mb")
            Am = wk.get([128, NB, 128], BF, "Am")
            Bm = wk.get([128, NB, 128], BF, "Bm")
            Tm = wk.get([128, NB, 128], BF, "Tm")
            Xm = wk.get([128, NB, 128], BF, "Xm")
            Yn = wk.get([128, NB, 128], BF, "Yn")
            Zn = wk.get([128, NB, 128], BF, "Zn")
            Eb = [wk.get([128, NB, 128], BF, "Eb0")]
            Fb = [wk.get([128, NB, 128], BF, "Fb0")]
            QKm = Rm
            QKmT = wk.get([128, NB, 128], BF, "QKmT")
            nwT = qs_tok
            o_tok = ks_tok
            vn = [wk.get([128, 128], BF, f"vn{i}") for i in range(2)]
            S_bf = wk.get([128, 128], BF, "S_bf")
            sso = wk.get([128, NB], F32, "sso")
            rso = wk.get([128, NB], F32, "rso")

            def stage2(h):
                par = h % 4
                qsT, ksT, vsT, zsT = s1[par]
                steps = []
                beta_h = beta.ap[:, :, h]
                eg_h = egkl.ap[:, :, h]
                ek_h = egkl.ap[:, :, 16 + h]

                def sa():
                    for src, dst, ssx in ((ksT, ks_tok, ssk), (qsT, qs_tok, ssq)):
                        bk = small()
                        for n in range(NB):
                            mm(bk.ap[:, nsl(n)], src.ap[:, nsl(n)], identb, True, True, (src.b, cb.b), (bk.b,))
                        cp(dst.ap, bk.ap[:, 0:TB].rearrange("p (a b) -> p a b", a=NB), (bk.b,), (dst.b,), eng="act")
                        tt(sqs.ap, dst.ap, dst.ap, ALU.mult, (dst.b,), (sqs.b,))
                        P.op("dve", lambda v, o=ssx.ap, i=sqs.ap: v.tensor_reduce(out=o, in_=i, op=ALU.add, axis=AX.X),
                             (sqs.b,), (ssx.b,))
                    bk = small()
                    for n in range(NB):
                        mm(bk.ap[:, nsl(n)], vsT.ap[:, nsl(n)], identb, True, True, (vsT.b, cb.b), (bk.b,))
                    tt(vb.ap, bk.ap[:, 0:TB].rearrange("p (a b) -> p a b", a=NB), bc(beta_h, 128), ALU.mult,
                       (bk.b, beta.b), (vb.b,))

                def sb():
                    rsqrt_small(rs_k.ap, ssk.ap, (ssk.b,), (rs_k.b,), ltmp)
                    rsqrt_small(rs_q.ap, ssq.ap, (ssq.b,), (rs_q.b,), ltmp, post_bias=-0.5 * math.log(128.0))
                    tt(c_t.ap, rs_k.ap, beta_h, ALU.mult, (rs_k.b, beta.b), (c_t.b,))
                    tt(c_kbg.ap, c_t.ap, eg_h, ALU.mult, (c_t.b, egkl.b), (c_kbg.b,))
                    tt(c_ke.ap, rs_k.ap, ek_h, ALU.mult, (rs_k.b, egkl.b), (c_ke.b,))
                    tt(c_qd.ap, rs_q.ap, eg_h, ALU.mult, (rs_q.b, egkl.b), (c_qd.b,))
                    idb = bcm(identb, NB)
                    tt(Dk.ap, idb, bc(rs_k.ap, 128), ALU.mult, (cb.b, rs_k.b), (Dk.b,), eng="pool")
                    tt(Dq.ap, idb, bc(rs_q.ap, 128), ALU.mult, (cb.b, rs_q.b), (Dq.b,), eng="pool")
                    tt(Dqd.ap, idb, bc(c_qd.ap, 128), ALU.mult, (cb.b, c_qd.b), (Dqd.b,), eng="pool")
                    tt(kbg.ap, ks_tok.ap, bc(c_kbg.ap, 128), ALU.mult, (ks_tok.b, c_kbg.b), (kbg.b,))
                    tt(ke.ap, ks_tok.ap, bc(c_ke.ap, 128), ALU.mult, (ks_tok.b, c_ke.b), (ke.b,))

                    tt(Rm.ap, bcm(ugtb, NB), bc(ghl.ap[:, :, h], 128), ALU.mult, (cb.b, ghl.b), (Rm.b,), eng="pool")
                    tt(SBm.ap, strict4.ap, bc(beta_h, 128), ALU.mult, (strict4.b, beta.b), (SBm.b,), eng="pool")

                def sc():
                    for src, dg, dst in ((ks_tok, Dk, knT), (qs_tok, Dq, qnT), (qs_tok, Dqd, qdT)):
                        bk = small()
                        for n in range(NB):
                            mm(bk.ap[:, nsl(n)], src.ap[:, n, :], dg.ap[:, n, :], True, True, (src.b, dg.b), (bk.b,))
                        cp(dst.ap, bk.ap[:, 0:TB], (bk.b,), (dst.b,), eng="act")
                    bk = small()
                    mm(bk.ap[:, 0:TB], uleb, Rm.ap.rearrange("p a b -> p (a b)"), True, False, (cb.b, Rm.b), (bk.b,))
                    mm(bk.ap[:, 0:TB], identb, negm4.ap.rearrange("p a b -> p (a b)"), False, True, (cb.b, negm4.b), (bk.b,))
                    act(Mfull.ap, bk.ap[:, 0:TB].rearrange("p (a b) -> p a b", a=NB), AF.Exp, (bk.b,), (Mfull.b,))
                    tt(Mb.ap, Mfull.ap, SBm.ap, ALU.mult, (Mfull.b, SBm.b), (Mb.b,))

                def sd():
                    bk = small()
                    for n in range(NB):
                        mm(bk.ap[:, nsl(n)], knT.ap[:, nsl(n)], knT.ap[:, nsl(n)], True, True, (knT.b,), (bk.b,))
                    tt(Am.ap, bk.ap[:, 0:TB].rearrange("p (a b) -> p a b", a=NB), Mb.ap, ALU.mult, (bk.b, Mb.b), (Am.b,))
                    bk = small()
                    for n in range(NB):
                        mm(bk.ap[:, nsl(n)], qnT.ap[:, nsl(n)], knT.ap[:, nsl(n)], True, True, (qnT.b, knT.b), (bk.b,))
                    tt(QKm.ap, bk.ap[:, 0:TB].rearrange("p (a b) -> p a b", a=NB), Mfull.ap, ALU.mult, (bk.b, Mfull.b), (QKm.b,))

                def se():
                    bk = small()
                    for n in range(NB):
                        mm(bk.ap[:, nsl(n)], Am.ap[:, n, :], identb, True, True, (Am.b, cb.b), (bk.b,))
                    cp(Bm.ap, bk.ap[:, 0:TB].rearrange("p (a b) -> p a b", a=NB), (bk.b,), (Bm.b,), eng="act")
                    bk = small()
                    for n in range(NB):
                        mm(bk.ap[:, nsl(n)], QKm.ap[:, n, :], identb, True, True, (QKm.b, cb.b), (bk.b,))
                    cp(QKmT.ap, bk.ap[:, 0:TB].rearrange("p (a b) -> p a b", a=NB), (bk.b,), (QKmT.b,), eng="act")

                def st0():
                    tt(Eb[0].ap, Am.ap, bcm(cb.ap[:, 6, :], NB), ALU.mult, (Am.b, cb.b), (Eb[0].b,), eng="pool")
                    tt(Fb[0].ap, Bm.ap, bcm(cb.ap[:, 13, :], NB), ALU.mult, (Bm.b, cb.b), (Fb[0].b,), eng="pool")
                    tt(Tm.ap, bcm(identb, NB), Eb[0].ap, ALU.subtract, (cb.b, Eb[0].b), (Tm.b,))
                    tt(Xm.ap, bcm(identb, NB), Fb[0].ap, ALU.subtract, (cb.b, Fb[0].b), (Xm.b,))

                def mkstage(sg):
                    def f():
                        E = Eb[sg % len(Eb)]
                        F = Fb[sg % len(Fb)]
                        last = sg == 6
                        v3 = lambda bk: bk.ap[:, 0:TB].rearrange("p (a b) -> p a b", a=NB)
                        tt(E.ap, Am.ap, bcm(cb.ap[:, 6 + sg, :], NB), ALU.mult, (Am.b, cb.b), (E.b,), eng="pool")
                        if not last:
                            tt(F.ap, Bm.ap, bcm(cb.ap[:, 13 + sg, :], NB), ALU.mult, (Bm.b, cb.b), (F.b,), eng="pool")
                        bY = small()
                        for n in range(NB):
                            mm(bY.ap[:, nsl(n)], E.ap[:, n, :], Xm.ap[:, n, :], True, True, (E.b, Xm.b), (bY.b,))
                        act(Yn.ap, v3(bY), AF.Identity, (bY.b,), (Yn.b,), scale=-1.0)
                        if not last:
                            bZ = small()
                            for n in range(NB):
                                mm(bZ.ap[:, nsl(n)], F.ap[:, n, :], Tm.ap[:, n, :], True, True, (F.b, Tm.b), (bZ.b,))
                            act(Zn.ap, v3(bZ), AF.Identity, (bZ.b,), (Zn.b,), scale=-1.0)
                        bX = small()
                        for n in range(NB):
                            mm(bX.ap[:, nsl(n)], identb, Xm.ap[:, n, :], True, False, (cb.b, Xm.b), (bX.b,))
                            mm(bX.ap[:, nsl(n)], Tm.ap[:, n, :], Yn.ap[:, n, :], False, True, (Tm.b, Yn.b), (bX.b,))
                        if not last:
                            bT = small()
                            for n in range(NB):
                                mm(bT.ap[:, nsl(n)], identb, Tm.ap[:, n, :], True, False, (cb.b, Tm.b), (bT.b,))
                                mm(bT.ap[:, nsl(n)], Xm.ap[:, n, :], Zn.ap[:, n, :], False, True, (Xm.b, Zn.b), (bT.b,))
                        cp(Xm.ap, v3(bX), (bX.b,), (Xm.b,))
                        if not last:
                            cp(Tm.ap, v3(bT), (bT.b,), (Tm.b,))
                    return f

                def sw_():
                    bk = small()
                    for n in range(NB):
                        mm(bk.ap[:, nsl(n)], kbg.ap[:, n, :], Xm.ap[:, n, :], True, True, (kbg.b, Xm.b), (bk.b,))
                    ts(nwT.ap, bk.ap[:, 0:TB].rearrange("p (a b) -> p a b", a=NB), -1.0, ALU.mult, (bk.b,), (nwT.b,))
                    cp(S_bf.ap, Sst.ap[:, h, :], (Sst_h[h],), (S_bf.b,), eng="act")

                def mkrec(n):
                    def f():
                        bk = small()
                        v = vn[n % 2]
                        mm(bk.ap[:, 0:128], Xm.ap[:, n, :], vb.ap[:, n, :], True, False, (Xm.b, vb.b), (bk.b,))
                        mm(bk.ap[:, 0:128], nwT.ap[:, n, :], S_bf.ap, False, True, (nwT.b, S_bf.b), (bk.b,))
                        cp(v.ap, bk.ap[:, 0:128], (bk.b,), (v.b,), eng="act")
                        mm(bk.ap[:, 128:256], ke.ap[:, n, :], v.ap, True, True, (ke.b, v.b), (bk.b,))
                        mm(bk.ap[:, 256:384], qdT.ap[:, nsl(n)], S_bf.ap, True, False, (qdT.b, S_bf.b), (bk.b,))
                        mm(bk.ap[:, 256:384], QKmT.ap[:, n, :], v.ap, False, True, (QKmT.b, v.b), (bk.b,))
                        stt(Sst.ap[:, h, :], Sst.ap[:, h, :], egkl.ap[:, n, 32 + h:33 + h], bk.ap[:, 128:256], ALU.mult, ALU.add,
                            (Sst_h[h], egkl.b, bk.b), (Sst_h[h],))
                        cp(S_bf.ap, Sst.ap[:, h, :], (Sst_h[h],), (S_bf.b,), eng="act")
                        cp(o_tok.ap[:, n, :], bk.ap[:, 256:384], (bk.b,), (o_tok.b,))
                        act(ojunk.ap, bk.ap[:, 256:384], AF.Square, (bk.b,), (ojunk.b, sso.b), accum_out=sso.ap[:, n:n + 1])
                    return f

                def sp_():
                    rsqrt_small(rso.ap, sso.ap, (sso.b,), (rso.b,), ltmp, scale=1.0 / 128)
                    tt(Do.ap, bcm(identb, NB), bc(rso.ap, 128), ALU.mult, (cb.b, rso.b), (Do.b,), eng="pool")
                    bk = small()
                    for n in range(NB):
                        mm(bk.ap[:, nsl(n)], o_tok.ap[:, n, :], Do.ap[:, n, :], True, True, (o_tok.b, Do.b), (bk.b,))
                    stt(oaT.ap[:, h, :], bk.ap[:, 0:TB], smc(C_GDN), zsT.ap, ALU.mult, ALU.mult, (bk.b, sm.b, zsT.b), (oaT.b,))

                steps += [sa, sb, sc, sd, se, st0]
                steps += [mkstage(sg) for sg in range(1, 7)]
                steps += [sw_]
                steps += [mkrec(n) for n in range(NB)]
                steps += [sp_]
                return steps

            return stage2

        envs = [make_env(0), make_env(1)]

        def interleave(a, b):
            out = []
            la, lb = len(a), len(b)
            if la == 0:
                return list(b)
            pos = [int((i + 0.5) * lb / la) for i in range(la)]
            ai = 0
            for j in range(lb + 1):
                while ai < la and pos[ai] == j:
                    out.append(a[ai])
                    ai += 1
                if j < lb:
                    out.append(b[j])
            return out

        pre_slots = {}

        s1_done = {}

        def prefetch(hh):
            if hh < H:
                assert hh < 2 or s1_done.get(hh - 2, 0) == 4, ("prefetch before stage1 done", hh)
                pre_slots[hh] = [load_w(w_in_d[:, off + hh * 128:off + (hh + 1) * 128])
                                 for off in (OFF_Q, OFF_K, OFF_V, OFF_ZA)]

        def zipsteps(a, b):
            out = []
            for i in range(max(len(a), len(b))):
                if i < len(a):
                    out.append(a[i])
                if i < len(b):
                    out.append(b[i])
            return out

        prefetch(0)
        prefetch(1)
        for f in stage1(0) + stage1(1):
            f()
        seqs = [[], []]
        for hh in range(H):
            st = envs[hh % 2](hh)
            seqs[hh % 2] += [(hh, i, f) for i, f in enumerate(st)]
        nst = len(seqs[0]) // (H // 2)
        off = nst // 2 if KOFF < 0 else KOFF
        merged = []
        for i in range(len(seqs[0]) + off):
            if i < len(seqs[0]):
                merged.append(seqs[0][i])
            j = i - off
            if 0 <= j < len(seqs[1]):
                merged.append(seqs[1][j])
        startpos = {}
        for pos, (hh, i, f) in enumerate(merged):
            if i == 0:
                startpos[hh] = pos
        inserts = {}
        for hh in range(2, H):
            fl = stage1(hh)
            w1 = startpos[hh] - 4
            w0 = max(0, w1 - KWIN)
            for q_, f in enumerate(fl):
                pos = w0 + int(q_ * (w1 - w0) / len(fl))
                inserts.setdefault(pos, []).append(f)
            ppos = max(0, w0 - nst // 2)
            inserts.setdefault(ppos, []).insert(0, (lambda hx=hh: prefetch(hx)))
        def mk_gate(dtile):
            def f():
                sg_ = NW
                dma("pool", wpool.ap[:, sg_, :, :],
                    w_in_d[:, OFF_GATE + dtile * 128:OFF_GATE + (dtile + 1) * 128].rearrange("(kc p) c -> p kc c", p=128),
                    (), (wslot[sg_],), f"w{sg_}")
                bk = big()
                for kc in range(KC):
                    mm(bk.ap[:, 0:TB], wpool.ap[:, sg_, kc, :], hT.ap[:, kc, :], kc == 0, kc == KC - 1, HT + (wslot[sg_],), (bk.b,))
                act(minT.ap[:, dtile, :], bk.ap[:, 0:TB], AF.Sigmoid, (bk.b, sm.b), (gak[dtile],), bias=smc(C_BGATE + dtile))
            return f

        gak = [Buf(f"ga{c}") for c in range(KC)]
        for dtile in range(KC):
            gpos = min(len(merged) - 1, 8 + dtile * (len(merged) // KC))
            inserts.setdefault(gpos, []).append(mk_gate(dtile))
        done_pf = set()
        for pos, (hh, i, f) in enumerate(merged):
            for g in inserts.get(pos, []):
                g()
            f()

        if dbg and blk == NBLK - 1:
            P.barrier()
            dt_ = wk.get([128, 16, TB], F32, "dbgt")
            cp(dt_.ap, oaT.ap, (oaT.b,), (dt_.b,))
            dma("sp", dbg_d[:, 0, :, :], dt_.ap, (dt_.b,), (), "dbg")

        if KSTOP < 2:
            continue
        P.barrier()
        wk.reset()
        DW = [wk.get([128, CK, 128], BF, f"DW{i}") for i in range(2)]
        upre = [wk.get([128, TB + 30], BF, f"upre{i}") for i in range(2)]
        sgB = wk.get([128, TB], F32, "sgB")
        usq = [wk.get([128, TB], BF, f"usq{i}") for i in range(2)]
        acc1 = wk.get([128, TB], F32, "acc1")
        acc2 = wk.get([128, TB], F32, "acc2")
        meanB = wk.get([128, TB], F32, "meanB")
        varB = wk.get([128, TB], F32, "varB")
        rstdB = wk.get([128, TB], F32, "rstdB")
        nmrB = wk.get([128, TB], F32, "nmrB")
        szb = wk.get([128, TB], BF, "szb")
        t1 = wk.get([128, TB], F32, "t1")
        t3 = wk.get([128, TB], BF, "t3")
        ubk = [Buf(f"ub{c}") for c in range(KC)]
        memset(acc1.ap, 0.0, (acc1.b,))
        memset(acc2.ap, 0.0, (acc2.b,))
        def b_inproj(ct):
            up, dw = upre[ct % 2], DW[ct % 2]
            sa_ = load_w(w_in_d[:, OFF_GLU + ct * 128:OFF_GLU + (ct + 1) * 128])
            sb_ = load_w(w_in_d[:, OFF_GLU + D + ct * 128:OFF_GLU + D + (ct + 1) * 128])
            tt(dw.ap, bcm(identb, CK), bc(sm.ap[:, C_WDW + ct * CK:C_WDW + (ct + 1) * CK], 128), ALU.mult,
               (cb.b, sm.b), (dw.b,), eng="pool")
            bA = big()
            for kc in range(KC):
                mm(bA.ap[:, 0:TB], wpool.ap[:, sa_, kc, :], hT.ap[:, kc, :], kc == 0, kc == KC - 1, HT + (wslot[sa_],), (bA.b,))
            bB = big()
            for kc in range(KC):
                mm(bB.ap[:, 0:TB], wpool.ap[:, sb_, kc, :], hT.ap[:, kc, :], kc == 0, kc == KC - 1, HT + (wslot[sb_],), (bB.b,))
            act(sgB.ap, bB.ap[:, 0:TB], AF.Sigmoid, (bB.b,), (sgB.b,))
            cp(up.ap[:, 0:30], tails_u.ap[:, ct, :], (tails_u.b,), (up.b,))
            tt(up.ap[:, 30:30 + TB], bA.ap[:, 0:TB], sgB.ap, ALU.mult, (bA.b, sgB.b), (up.b,))
            cp(tails_u.ap[:, ct, :], up.ap[:, TB:TB + 30], (up.b,), (tails_u.b,))

        def b_conv(ct):
            up, dw, uq = upre[ct % 2], DW[ct % 2], usq[ct % 2]
            bC = big()
            for k in range(CK):
                mm(bC.ap[:, 0:TB], dw.ap[:, k, :], up.ap[:, k:k + TB], k == 0, k == CK - 1, (dw.b, up.b), (bC.b,))
            act(ubT.ap[:, ct, :], bC.ap[:, 0:TB], AF.Identity, (bC.b, sm.b), (ubk[ct],), bias=smc(C_BDW + ct))
            act(uq.ap, bC.ap[:, 0:TB], AF.Square, (bC.b, sm.b), (uq.b,), bias=smc(C_BDW + ct))

        def b_stats(ct):
            uq = usq[ct % 2]
            bS = small()
            mm(bS.ap[:, 0:TB], onesb, ubT.ap[:, ct, :], True, True, (cb.b, ubk[ct]), (bS.b,))
            tt(acc1.ap, acc1.ap, bS.ap[:, 0:TB], ALU.add, (acc1.b, bS.b), (acc1.b,))
            bS2 = small()
            mm(bS2.ap[:, 0:TB], onesb, uq.ap, True, True, (cb.b, uq.b), (bS2.b,))
            tt(acc2.ap, acc2.ap, bS2.ap[:, 0:TB], ALU.add, (acc2.b, bS2.b), (acc2.b,))

        b_inproj(0)
        for ct in range(KC):
            if ct + 1 < KC:
                b_inproj(ct + 1)
            b_conv(ct)
            if ct > 0:
                b_stats(ct - 1)
        b_stats(KC - 1)
        ts(meanB.ap, acc1.ap, 1.0 / D, ALU.mult, (acc1.b,), (meanB.b,))
        tt(varB.ap, meanB.ap, meanB.ap, ALU.mult, (meanB.b,), (varB.b,))
        stt(varB.ap, acc2.ap, 1.0 / D, varB.ap, ALU.mult, ALU.subtract, (acc2.b, varB.b), (varB.b,))
        act(t1.ap, varB.ap, AF.Ln, (varB.b,), (t1.b,), bias=EPS)
        act(rstdB.ap, t1.ap, AF.Exp, (t1.b,), (rstdB.b,), scale=-0.5)
        stt(nmrB.ap, meanB.ap, -1.0, rstdB.ap, ALU.mult, ALU.mult, (meanB.b, rstdB.b), (nmrB.b,))
        for ct in range(KC):
            sz = load_w(w_in_d[:, OFF_ZB + ct * 128:OFF_ZB + (ct + 1) * 128])
            bZ = big()
            for kc in range(KC):
                mm(bZ.ap[:, 0:TB], wpool.ap[:, sz, kc, :], hT.ap[:, kc, :], kc == 0, kc == KC - 1, HT + (wslot[sz],), (bZ.b,))
            act(szb.ap, bZ.ap[:, 0:TB], AF.Silu, (bZ.b,), (szb.b,))
            tt(t1.ap, ubT.ap[:, ct, :], rstdB.ap, ALU.mult, (ubk[ct], rstdB.b), (t1.b,))
            tt(t1.ap, t1.ap, nmrB.ap, ALU.add, (t1.b, nmrB.b), (t1.b,))
            act(t3.ap, t1.ap, AF.Silu, (t1.b, sm.b), (t3.b,), scale=smc(C_LNG + ct), bias=smc(C_LNB + ct))
            tt(ubT.ap[:, ct, :], t3.ap, szb.ap, ALU.mult, (t3.b, szb.b), (ubk[ct],))
        UB = tuple(ubk)

        if dbg and blk == NBLK - 1:
            P.barrier()
            dt_ = wk.get([128, 16, TB], F32, "dbgt")
            cp(dt_.ap, ubT.ap, UB, (dt_.b,))
            dma("sp", dbg_d[:, 1, :, :], dt_.ap, (dt_.b,), (), "dbg")

        if KSTOP < 3:
            continue
        P.barrier()
        wk.reset()
        gAf = wk.get([128, TB], F32, "gAf")
        gBf = wk.get([128, TB], F32, "gBf")
        m1 = wk.get([128, TB], F32, "m1")
        m2 = wk.get([128, TB], F32, "m2")
        mik = [Buf(f"mi{c}") for c in range(KC)]
        for dtile in range(KC):
            cs = slice(dtile * 128, (dtile + 1) * 128)
            s_a = load_w(w_bra_d[:, cs])
            s_b = load_w(w_brb_d[:, cs])
            s_gb = load_w(w_in_d[:, OFF_GATE + D + dtile * 128:OFF_GATE + D + (dtile + 1) * 128])
            b1 = big()
            for kc in range(KC):
                mm(b1.ap[:, 0:TB], wpool.ap[:, s_a, kc, :], oaT.ap[:, kc, :], kc == 0, kc == KC - 1, (oaT.b, wslot[s_a]), (b1.b,))
            tt(m1.ap, b1.ap[:, 0:TB], minT.ap[:, dtile, :], ALU.mult, (b1.b, gak[dtile]), (m1.b,))
            b3 = big()
            for kc in range(KC):
                mm(b3.ap[:, 0:TB], wpool.ap[:, s_b, kc, :], ubT.ap[:, kc, :], kc == 0, kc == KC - 1, UB + (wslot[s_b],), (b3.b,))
            b4 = big()
            for kc in range(KC):
                mm(b4.ap[:, 0:TB], wpool.ap[:, s_gb, kc, :], hT.ap[:, kc, :], kc == 0, kc == KC - 1, HT + (wslot[s_gb],), (b4.b,))
            act(gBf.ap, b4.ap[:, 0:TB], AF.Sigmoid, (b4.b, sm.b), (gBf.b,), bias=smc(C_BGATE + 16 + dtile))
            tt(m2.ap, b3.ap[:, 0:TB], gBf.ap, ALU.mult, (b3.b, gBf.b), (m2.b,))
            tt(minT.ap[:, dtile, :], m1.ap, m2.ap, ALU.add, (m1.b, m2.b, gak[dtile]), (mik[dtile], gak[dtile]))
        MI = tuple(mik)

        if dbg and blk == NBLK - 1:
            P.barrier()
            dt_ = wk.get([128, 16, TB], F32, "dbgt")
            cp(dt_.ap, minT.ap, MI, (dt_.b,))
            dma("sp", dbg_d[:, 2, :, :], dt_.ap, (dt_.b,), (), "dbg")

        if KSTOP < 4:
            continue
        P.barrier()
        wk.reset()
        xt = [wk.get([128, D], F32, f"xtd{i}") for i in range(2)]
        grow = wk.get([128, D], F32, "grow")
        tmpD = wk.get([128, D], F32, "tmpD")
        junkD = wk.get([128, D], BF, "junkD")
        ss1 = wk.get([128, NB], F32, "ss1")
        l1 = wk.get([128, NB], F32, "l1")
        rstd1 = wk.get([128, NB], F32, "rstd1")
        mxk = [Buf(f"mx{n}") for n in range(NB)]
        dma("sp", grow.ap, rows_d[:, 0:D], (), (grow.b,), "grow")
        for cg in range(4):
            s0 = load_wide(w_out_d[:, cg * 512:(cg + 1) * 512])
            for n in range(NB):
                bk = big()
                for kc in range(KC):
                    mm(bk.ap[:, 0:512], minT.ap[:, kc, nsl(n)], wpool.ap[:, s0:s0 + 4, kc, :], kc == 0, kc == KC - 1,
                       MI + tuple(wslot[s0:s0 + 4]), (bk.b,))
                cp(mixed.ap[:, n, cg * 512:(cg + 1) * 512], bk.ap[:, 0:512], (bk.b,), (mxk[n],),
                   eng=("act" if (n + cg) % 2 else "dve"))
        for n in range(NB):
            xs = xt[n % 2]
            dma("sp", xs.ap, x_d[t0 + n * 128:t0 + (n + 1) * 128, :], (), (xs.b,), f"xt{n % 2}")
            act(junkD.ap, mixed.ap[:, n, :], AF.Square, (mxk[n],), (junkD.b, ss1.b), accum_out=ss1.ap[:, n:n + 1])
            act(l1.ap[:, n:n + 1], ss1.ap[:, n:n + 1], AF.Ln, (ss1.b,), (l1.b,), scale=1.0 / D, bias=EPS)
            act(rstd1.ap[:, n:n + 1], l1.ap[:, n:n + 1], AF.Exp, (l1.b,), (rstd1.b,), scale=-0.5)
            stt(tmpD.ap, mixed.ap[:, n, :], rstd1.ap[:, n:n + 1], grow.ap, ALU.mult, ALU.mult, (mxk[n], rstd1.b, grow.b), (tmpD.b,))
            tt(mixed.ap[:, n, :], tmpD.ap, xs.ap, ALU.add, (tmpD.b, xs.b), (mxk[n],))

        if KSTOP < 5:
            continue
        P.barrier()
        wk.reset()
        grow = wk.get([128, D], F32, "growE")
        x1b = wk.get([128, D], BF, "x1b")
        ptile = wk.get([128, PLE], F32, "ptile")
        pbt = wk.get([128, PLE], BF, "pbt")
        pT = wk.get([128, 2, TB], BF, "pT")
        wpp = wk.get([128, 2, D], BF, "wpp")
        sgE = wk.get([128, 512], F32, "sgE")
        tmpE = [wk.get([128, D], F32, f"tmpE{i}") for i in range(2)]
        junkE = x1b
        vbuf = wk.get([128, NB, D], F32, "vbuf")
        ss2 = wk.get([128, NB], F32, "ss2")
        l2 = wk.get([128, NB], F32, "l2")
        rstd2 = wk.get([128, NB], F32, "rstd2")
        x1k = [Buf(f"x1T{n}") for n in range(NB)]
        vbk = [Buf(f"vb{n}") for n in range(NB)]
        dma("sp", grow.ap, rows_d[:, D:2 * D], (), (grow.b,), "grow")
        for j in range(4):
            dma("pool", wpp.ap[:, :, j * 512:(j + 1) * 512],
                w_pp_d[:, j * 512:(j + 1) * 512].rearrange("(kc p) c -> p kc c", p=128), (), (wpp.b,), "wpp")
        for n in range(NB):
            cp(x1b.ap, mixed.ap[:, n, :], (mxk[n],), (x1b.b,), eng="act")
            for g in range(4):
                bk = big()
                for j in range(4):
                    kc = g * 4 + j
                    mm(bk.ap[:, j * 128:(j + 1) * 128], x1b.ap[:, kc * 128:(kc + 1) * 128], identb, True, True, (x1b.b, cb.b), (bk.b,))
                cp(x1T.ap[:, g * 4:g * 4 + 4, nsl(n)], bk.ap.rearrange("p (a b) -> p a b", a=4), (bk.b,), (x1k[n],),
                   eng=("act" if g % 2 else "dve"))
            dma("sp", ptile.ap, p_d[t0 + n * 128:t0 + (n + 1) * 128, :], (), (ptile.b,), "ptile")
            cp(pbt.ap, ptile.ap, (ptile.b,), (pbt.b,))
            bk = small()
            for j in range(2):
                mm(bk.ap[:, j * 128:(j + 1) * 128], pbt.ap[:, j * 128:(j + 1) * 128], identb, True, True, (pbt.b, cb.b), (bk.b,))
            cp(pT.ap[:, :, nsl(n)], bk.ap[:, 0:256].rearrange("p (a b) -> p a b", a=2), (bk.b,), (pT.b,), eng="act")
        for cg in range(4):
            s0 = load_wide(w_pg_d[:, cg * 512:(cg + 1) * 512])
            for n in range(NB):
                bG = big()
                for kc in range(KC):
                    mm(bG.ap[:, 0:512], x1T.ap[:, kc, nsl(n)], wpool.ap[:, s0:s0 + 4, kc, :], kc == 0, kc == KC - 1,
                       (x1k[n],) + tuple(wslot[s0:s0 + 4]), (bG.b,))
                bE = big()
                for kc in range(2):
                    mm(bE.ap[:, 0:512], pT.ap[:, kc, nsl(n)], wpp.ap[:, kc, cg * 512:(cg + 1) * 512], kc == 0, kc == 1,
                       (pT.b, wpp.b), (bE.b,))
                act(sgE.ap, bG.ap[:, 0:512], AF.Sigmoid, (bG.b,), (sgE.b,))
                tt(vbuf.ap[:, n, cg * 512:(cg + 1) * 512], bE.ap[:, 0:512], sgE.ap, ALU.mult, (bE.b, sgE.b), (vbk[n],))
        for n in range(NB):
            tE = tmpE[n % 2]
            act(junkE.ap, vbuf.ap[:, n, :], AF.Square, (vbk[n],), (junkE.b, ss2.b), accum_out=ss2.ap[:, n:n + 1])
            act(l2.ap[:, n:n + 1], ss2.ap[:, n:n + 1], AF.Ln, (ss2.b,), (l2.b,), scale=1.0 / D, bias=EPS)
            act(rstd2.ap[:, n:n + 1], l2.ap[:, n:n + 1], AF.Exp, (l2.b,), (rstd2.b,), scale=-0.5)
            stt(tE.ap, vbuf.ap[:, n, :], rstd2.ap[:, n:n + 1], grow.ap, ALU.mult, ALU.mult, (vbk[n], rstd2.b, grow.b), (tE.b,))
            tt(tE.ap, tE.ap, mixed.ap[:, n, :], ALU.add, (tE.b, mxk[n]), (tE.b,))
            dma("sp", out_d[t0 + n * 128:t0 + (n + 1) * 128, :], tE.ap, (tE.b,), (), f"st{n % 2}")
        P.barrier()

    P.barrier()
    P.finalize()
    esem = {n: es.enter_context(nc.semaphore(f"e_{n}")) for n in Prog.ENGS}
    dsem = {k: es.enter_context(nc.semaphore(f"d_{k}")) for k in P.dcount}
    block = es.enter_context(nc.Block())

    @block.tensor
    def _(t):
        P.replay("pe", t, esem, dsem)

    @block.scalar
    def _(s):
        P.replay("act", s, esem, dsem)

    @block.vector
    def _(v):
        P.replay("dve", v, esem, dsem)

    @block.gpsimd
    def _(g):
        P.replay("pool", g, esem, dsem)

    @block.sync
    def _(sy):
        P.replay("sp", sy, esem, dsem)

    es.close()
    return nc


def host_consts(g_pre, b_gate, w_conv_qkv, a_log, dt_bias, g_dn_out, w_dw, b_dw, ln_g, ln_b, g_post, g_ple):
    sm = np.zeros((128, NS), np.float32)
    sm[:, C_GPRE:C_GPRE + 16] = g_pre[0].reshape(16, 128).T
    sm[:, C_BGATE:C_BGATE + 32] = b_gate[0].reshape(32, 128).T
    sm[:, C_WCONV:C_WCONV + 192] = w_conv_qkv[0].reshape(4, 48, 128).transpose(2, 1, 0).reshape(128, 192)
    sm[:, C_WDW:C_WDW + 496] = w_dw[0].reshape(CK, 16, 128).transpose(2, 1, 0).reshape(128, 496)
    sm[:, C_BDW:C_BDW + 16] = b_dw[0].reshape(16, 128).T
    sm[:, C_LNG:C_LNG + 16] = ln_g[0].reshape(16, 128).T
    sm[:, C_LNB:C_LNB + 16] = ln_b[0].reshape(16, 128).T
    sm[:, C_GDN] = g_dn_out[0]
    sm[:, C_ALOG:C_ALOG + 16] = np.broadcast_to(a_log[0][None, :], (128, 16))
    sm[:, C_DTB:C_DTB + 16] = np.broadcast_to(dt_bias[0][None, :], (128, 16))
    rows = np.zeros((128, 2 * D), np.float32)
    rows[:, 0:D] = np.broadcast_to(g_post[0][None, :], (128, D))
    rows[:, D:] = np.broadcast_to(g_ple[0][None, :], (128, D))
    i = np.arange(128)
    cst = np.zeros((128, NCST, 128), np.float32)
    cst[:, 0] = np.eye(128)
    cst[:, 1] = (i[:, None] <= i[None, :])
    cst[:, 2] = (i[:, None] > i[None, :])
    cst[:, 3] = 1.0
    cst[:, 4] = NEG * (i[:, None] < i[None, :])
    cst[:, 5] = (i[:, None] > i[None, :])
    for sg in range(7):
        bsz = 1 << sg
        same = (i[:, None] // (2 * bsz)) == (i[None, :] // (2 * bsz))
        mE = same & ((i[:, None] % (2 * bsz)) >= bsz) & ((i[None, :] % (2 * bsz)) < bsz)
        cst[:, 6 + sg] = mE
        cst[:, 13 + sg] = mE.T
    return sm, rows, cst.reshape(128, NCST * 128)


_NC_CACHE = {}


def kernel(x, p, g_pre, w_in, b_gate, w_conv_qkv, a_log, dt_bias, g_dn_out, w_dw, b_dw,
           ln_g, ln_b, w_br_a, w_br_b, w_out, g_post, w_ple_gate, w_ple_proj, g_ple):
    x = np.asarray(x, np.float32)
    p = np.asarray(p, np.float32)
    B, S, _ = x.shape
    TB = 512 if S % 512 == 0 else 128
    f = lambda a: np.ascontiguousarray(np.asarray(a, np.float32))
    sm, rows, cst = host_consts(*[np.asarray(a, np.float32) for a in
                                  (g_pre, b_gate, w_conv_qkv, a_log, dt_bias, g_dn_out, w_dw, b_dw, ln_g, ln_b, g_post, g_ple)])
    key = (S, TB)
    if key not in _NC_CACHE:
        _NC_CACHE[key] = build_nc(S, TB)
    nc = _NC_CACHE[key]
    shared = {"w_in": f(w_in[0]), "w_br_a": f(w_br_a[0]), "w_br_b": f(w_br_b[0]), "w_out": f(w_out[0]),
              "w_ple_gate": f(w_ple_gate[0]), "w_ple_proj": f(w_ple_proj[0]), "smalls": sm, "rows": rows, "cst": cst}
    in_maps = []
    for b in range(B):
        m = dict(shared)
        m["x"] = f(x[b])
        m["p"] = f(p[0, b])
        in_maps.append(m)
    res = run_bass_kernel_spmd(nc, in_maps, core_ids=list(range(B)))
    return np.stack([np.asarray(r["out"], np.float32) for r in res.results], axis=0)
```

```python
import math
from contextlib import ExitStack

import numpy as np
import concourse.bass as bass
import concourse.mybir as mybir
from concourse.bass_utils import run_bass_kernel_spmd

F32 = mybir.dt.float32
BF = mybir.dt.bfloat16
AF = mybir.ActivationFunctionType
ALU = mybir.AluOpType
AX = mybir.AxisListType

D = 2048
KC = 16
H = 16
PLE = 256
INC = 18464
OFF_Q, OFF_K, OFF_V, OFF_ZA, OFF_BG, OFF_GLU, OFF_ZB, OFF_GATE = 0, 2048, 4096, 6144, 8192, 8224, 12320, 14368
EPS = 1e-6
CK = 31

C_GPRE = 0
C_BGATE = 16
C_WCONV = 48
C_WDW = 240
C_BDW = 736
C_LNG = 752
C_LNB = 768
C_GDN = 784
C_ALOG = 785
C_DTB = 801
NS = 820
NEG = -30000.0
NCST = 20
KSTOP = 9
MMG = 16
KPAIRS = 99
KSTEPS = 999
KOFF = -1
KWIN = 1


class Buf:
    __slots__ = ("w", "r", "name")

    def __init__(self, name=""):
        self.w = None
        self.r = {}
        self.name = name


class Rec:
    __slots__ = ("waits_c", "waits_d", "fn", "dma_sem", "needed", "val")

    def __init__(self, waits_c, waits_d, fn, dma_sem):
        self.waits_c = waits_c
        self.waits_d = waits_d
        self.fn = fn
        self.dma_sem = dma_sem
        self.needed = False
        self.val = 0


class EngState:
    def __init__(self, name):
        self.name = name
        self.ops = []
        self.seen_c = {}
        self.seen_d = {}
        self.last_c = -1


class Prog:
    ENGS = ("pe", "act", "dve", "pool", "sp")

    def __init__(self):
        self.E = {n: EngState(n) for n in self.ENGS}
        self.dcount = {}

    def op(self, eng, fn, reads=(), writes=(), dma_sem=None):
        e = self.E[eng]
        need_c = {}
        need_d = {}

        def add(tok, raw):
            if tok is None:
                return
            if tok[0] == "c":
                en, idx = tok[1], tok[2]
                if en == eng and dma_sem is None:
                    if eng == "pe":
                        return
                if e.seen_c.get(en, -1) >= idx:
                    return
                if need_c.get(en, -1) < idx:
                    need_c[en] = idx
            else:
                sk, val = tok[1], tok[2]
                if e.seen_d.get(sk, 0) >= val:
                    return
                if need_d.get(sk, 0) < val:
                    need_d[sk] = val

        for b in reads:
            add(b.w, True)
        for b in writes:
            add(b.w, False)
            for t in b.r.values():
                add(t, False)
        for en, idx in need_c.items():
            e.seen_c[en] = idx
            self.E[en].ops[idx].needed = True
        for sk, val in need_d.items():
            e.seen_d[sk] = val
        rec = Rec(need_c, need_d, fn, dma_sem)
        e.ops.append(rec)
        idx = len(e.ops) - 1
        if dma_sem is None:
            tok = ("c", eng, idx)
            key = eng
            e.last_c = idx
        else:
            self.dcount[dma_sem] = self.dcount.get(dma_sem, 0) + 16
            tok = ("d", dma_sem, self.dcount[dma_sem])
            key = ("d", dma_sem)
        for b in reads:
            b.r[key] = tok
        for b in writes:
            b.w = tok
            b.r = {}

    def barrier(self):
        for eng in self.ENGS:
            e = self.E[eng]
            need_c = {}
            need_d = {}
            for en in self.ENGS:
                o = self.E[en]
                if en == eng or o.last_c < 0:
                    continue
                if e.seen_c.get(en, -1) < o.last_c:
                    need_c[en] = o.last_c
                    e.seen_c[en] = o.last_c
                    o.ops[o.last_c].needed = True
            for sk, val in self.dcount.items():
                if e.seen_d.get(sk, 0) < val:
                    need_d[sk] = val
                    e.seen_d[sk] = val
            if need_c or need_d:
                e.ops.append(Rec(need_c, need_d, None, None))

    def finalize(self):
        for e in self.E.values():
            cum = 0
            for rec in e.ops:
                if rec.fn is not None and rec.dma_sem is None and rec.needed:
                    cum += 1
                    rec.val = cum

    def replay(self, eng, h, esem, dsem):
        e = self.E[eng]
        for rec in e.ops:
            for en, idx in rec.waits_c.items():
                h.wait_ge(esem[en], self.E[en].ops[idx].val)
            for sk, val in rec.waits_d.items():
                h.wait_ge(dsem[sk], val)
            if rec.fn is None:
                continue
            ins = rec.fn(h)
            if rec.dma_sem is not None:
                ins.then_inc(dsem[rec.dma_sem], 16)
            elif rec.needed:
                ins.then_inc(esem[eng], 1)


class T:
    __slots__ = ("ap", "b")

    def __init__(self, ap, name=""):
        self.ap = ap
        self.b = Buf(name)


def build_nc(S, TB, dbg=False):
    NB = TB // 128
    NBLK = S // TB
    assert TB % 128 == 0 and TB <= 512 and S % TB == 0
    nc = bass.Bass("TRN2", target_bir_lowering=False)
    x_d = nc.dram_tensor("x", [S, D], F32, kind="ExternalInput").ap()
    p_d = nc.dram_tensor("p", [S, PLE], F32, kind="ExternalInput").ap()
    w_in_d = nc.dram_tensor("w_in", [D, INC], F32, kind="ExternalInput").ap()
    w_bra_d = nc.dram_tensor("w_br_a", [D, D], F32, kind="ExternalInput").ap()
    w_brb_d = nc.dram_tensor("w_br_b", [D, D], F32, kind="ExternalInput").ap()
    w_out_d = nc.dram_tensor("w_out", [D, D], F32, kind="ExternalInput").ap()
    w_pg_d = nc.dram_tensor("w_ple_gate", [D, D], F32, kind="ExternalInput").ap()
    w_pp_d = nc.dram_tensor("w_ple_proj", [PLE, D], F32, kind="ExternalInput").ap()
    sm_d = nc.dram_tensor("smalls", [128, NS], F32, kind="ExternalInput").ap()
    rows_d = nc.dram_tensor("rows", [128, 2 * D], F32, kind="ExternalInput").ap()
    cst_d = nc.dram_tensor("cst", [128, NCST * 128], F32, kind="ExternalInput").ap()
    out_d = nc.dram_tensor("out", [S, D], F32, kind="ExternalOutput").ap()
    dbg_d = None
    if dbg:
        dbg_d = nc.dram_tensor("dbg", [128, 4, 16, TB], F32, kind="ExternalOutput").ap()

    P = Prog()
    NW = 8
    ARENA_F32 = 51200

    es = ExitStack()
    arena = es.enter_context(nc.sbuf_tensor("arena", [128, ARENA_F32], F32))
    psum = es.enter_context(nc.psum_tensor("psum", [128, 8, 512], F32))
    PS = [T(psum[:, i, :], f"ps{i}") for i in range(8)]

    class Alloc:
        def __init__(self, base, limit):
            self.base = base
            self.off = base
            self.limit = limit

        def reset(self):
            self.off = self.base

        def get(self, shape, dt, name=""):
            n = 1
            for s in shape[1:]:
                n *= s
            nbytes = n * (4 if dt == F32 else 2)
            nw = (nbytes + 3) // 4
            assert self.off + nw <= self.limit, (name, self.off, nw, self.limit)
            ap = arena[:, self.off:self.off + nw]
            self.off += nw
            if dt != F32:
                ap = ap.bitcast(dt)
                if nbytes % 4:
                    ap = ap[:, 0:n]
            if len(shape) == 3:
                ap = ap.rearrange("p (a b) -> p a b", a=shape[1])
            elif len(shape) == 4:
                ap = ap.rearrange("p (a b c) -> p a b c", a=shape[1], b=shape[2])
            return T(ap, name)

    pa = Alloc(0, ARENA_F32)
    cb = pa.get([128, NCST, 128], BF, "cb")
    identb, uleb, ugtb, onesb = cb.ap[:, 0, :], cb.ap[:, 1, :], cb.ap[:, 2, :], cb.ap[:, 3, :]
    negm4 = pa.get([128, NB, 128], BF, "negm4")
    strict4 = pa.get([128, NB, 128], F32, "strict4")
    sm = pa.get([128, NS], F32, "sm")
    negA = pa.get([128, 16], F32, "negA")
    tails_qkv = pa.get([128, 48, 3], F32, "tails_qkv")
    tails_u = pa.get([128, 16, 30], BF, "tails_u")
    Sst = pa.get([128, 16, 128], F32, "Sst")
    Sst_h = [Buf(f"S{h}") for h in range(H)]
    wpool = pa.get([128, NW, 16, 128], BF, "wpool")
    wslot = [Buf(f"w{i}") for i in range(NW)]
    arX0 = pa.off
    hT = pa.get([128, KC, TB], BF, "hT")
    oaT = pa.get([128, KC, TB], BF, "oaT")
    arX1 = pa.off
    arY0 = pa.off
    ubT = pa.get([128, KC, TB], BF, "ubT")
    arZ0 = pa.off
    minT = pa.get([128, KC, TB], BF, "minT")
    alX = Alloc(arX0, arX1)
    mixed = alX.get([128, NB, D], F32, "mixed")
    alY = Alloc(arY0, arZ0)
    x1T = alY.get([128, KC, TB], BF, "x1T")
    wk = Alloc(pa.off, ARENA_F32)

    wptr = [0]
    bigp = [0]
    smallp = [0]

    def big():
        b = PS[bigp[0] % 3]
        bigp[0] += 1
        return b

    def small():
        b = PS[3 + smallp[0] % 5]
        smallp[0] += 1
        return b

    def mm(out, lhsT, rhs, start, stop, R, W):
        P.op("pe", lambda t: t.matmul(out, lhsT=lhsT, rhs=rhs, start=start, stop=stop), R, W)

    def act(out, in_, func, R, W, **kw):
        P.op("act", lambda s: s.activation(out=out, in_=in_, func=func, **kw), R, W)

    def tt(out, in0, in1, op, R, W, eng="dve"):
        P.op(eng, lambda v: v.tensor_tensor(out=out, in0=in0, in1=in1, op=op), R, W)

    def ts(out, in0, s1, op0, R, W, s2=None, op1=None, eng="dve"):
        if op1 is None:
            P.op(eng, lambda v: v.tensor_scalar(out=out, in0=in0, scalar1=s1, scalar2=None, op0=op0), R, W)
        else:
            P.op(eng, lambda v: v.tensor_scalar(out=out, in0=in0, scalar1=s1, scalar2=s2, op0=op0, op1=op1), R, W)

    def stt(out, in0, scalar, in1, op0, op1, R, W):
        P.op("dve", lambda v: v.scalar_tensor_tensor(out=out, in0=in0, scalar=scalar, in1=in1, op0=op0, op1=op1), R, W)

    def cp(out, in_, R, W, eng="dve"):
        if eng == "act":
            P.op("act", lambda s: s.activation(out=out, in_=in_, func=AF.Copy), R, W)
        else:
            P.op(eng, lambda v: v.tensor_copy(out=out, in_=in_), R, W)

    def memset(ap, val, W):
        P.op("dve", lambda v: v.memset(ap, val), (), W)

    def dma(eng, out, in_, R, W, sem):
        P.op(eng, lambda q: q.dma_start(out=out, in_=in_), R, W, dma_sem=sem)

    def load_w(src, ncols=128, k=D):
        s = wptr[0] % NW
        wptr[0] += 1
        nk = k // 128
        dma("pool", wpool.ap[:, s, 0:nk, 0:ncols], src.rearrange("(kc p) c -> p kc c", p=128), (), (wslot[s],), f"w{s}")
        return s

    def load_wide(src512):
        while wptr[0] % 4:
            wptr[0] += 1
        s0 = wptr[0] % NW
        for j in range(4):
            load_w(src512[:, j * 128:(j + 1) * 128])
        return s0

    def rsqrt_small(out, in_, R, W, tmp, scale=1.0, post_bias=0.0):
        act(tmp.ap, in_, AF.Ln, R, (tmp.b,), scale=scale, bias=EPS)
        act(out, tmp.ap, AF.Exp, (tmp.b,), W, scale=-0.5, bias=post_bias)

    def bc(ap2, n):
        return ap2.unsqueeze(2).to_broadcast([128, ap2.shape[1], n])

    def bcm(ap2, a):
        return ap2.unsqueeze(1).to_broadcast([128, a, ap2.shape[1]])

    wk.reset()
    cstf = wk.get([128, NCST, 128], F32, "cstf")
    dma("sp", cstf.ap, cst_d.rearrange("p (a b) -> p a b", a=NCST), (), (cstf.b,), "cst")
    dma("sp", sm.ap, sm_d, (), (sm.b,), "sm")
    cp(cb.ap, cstf.ap, (cstf.b,), (cb.b,))
    cp(negm4.ap, bcm(cstf.ap[:, 4, :], NB), (cstf.b,), (negm4.b,))
    cp(strict4.ap, bcm(cstf.ap[:, 5, :], NB), (cstf.b,), (strict4.b,))
    act(negA.ap, sm.ap[:, C_ALOG:C_ALOG + 16], AF.Exp, (sm.b,), (negA.b,))
    ts(negA.ap, negA.ap, -1.0, ALU.mult, (negA.b,), (negA.b,))
    memset(tails_qkv.ap, 0.0, (tails_qkv.b,))
    memset(tails_u.ap, 0.0, (tails_u.b,))
    memset(Sst.ap, 0.0, [Sst.b] + Sst_h)
    P.barrier()

    def smc(c):
        return sm.ap[:, c:c + 1]

    for blk in range(NBLK):
        t0 = blk * TB
        wk.reset()
        xt = [wk.get([128, D], F32, f"xt{i}") for i in range(2)]
        xjunk = wk.get([128, D], BF, "xjunk")
        xn = wk.get([128, D], BF, "xn")
        ss0 = wk.get([128, NB], F32, "ss0")
        l0 = wk.get([128, NB], F32, "l0")
        rstd0 = wk.get([128, NB], F32, "rstd0")
        hTk = [Buf(f"hT{n}") for n in range(NB)]
        for n in range(NB):
            xs = xt[n % 2]
            dma("sp", xs.ap, x_d[t0 + n * 128:t0 + (n + 1) * 128, :], (), (xs.b,), f"xt{n % 2}")
            act(xjunk.ap, xs.ap, AF.Square, (xs.b,), (xjunk.b, ss0.b), accum_out=ss0.ap[:, n:n + 1])
            act(l0.ap[:, n:n + 1], ss0.ap[:, n:n + 1], AF.Ln, (ss0.b,), (l0.b,), scale=1.0 / D, bias=EPS)
            act(rstd0.ap[:, n:n + 1], l0.ap[:, n:n + 1], AF.Exp, (l0.b,), (rstd0.b,), scale=-0.5)
            ts(xn.ap, xs.ap, rstd0.ap[:, n:n + 1], ALU.mult, (xs.b, rstd0.b), (xn.b,))
            for g in range(4):
                bk = big()
                for j in range(4):
                    kc = g * 4 + j
                    mm(bk.ap[:, j * 128:(j + 1) * 128], xn.ap[:, kc * 128:(kc + 1) * 128], identb, True, True,
                       (xn.b, cb.b), (bk.b,))
                tt(hT.ap[:, g * 4:g * 4 + 4, n * 128:(n + 1) * 128], bk.ap.rearrange("p (a b) -> p a b", a=4),
                   bc(sm.ap[:, C_GPRE + g * 4:C_GPRE + g * 4 + 4], 128), ALU.mult, (bk.b, sm.b), (hTk[n],))
        HT = tuple(hTk)

        if KSTOP < 1:
            continue
        P.barrier()
        wk.reset()
        bgs = wk.get([128, NB, 32], F32, "bgs")
        beta = wk.get([128, NB, 16], F32, "beta")
        t16 = wk.get([128, NB, 16], F32, "t16")
        gg = wk.get([128, NB, 16], F32, "gg")
        ghl = wk.get([128, NB, 32], BF, "ghl")
        egkl = wk.get([128, NB, 48], F32, "egkl")
        sw = load_w(w_in_d[:, OFF_BG:OFF_BG + 32], ncols=32)
        for n in range(NB):
            bk = small()
            for kc in range(KC):
                mm(bk.ap[:, 0:32], hT.ap[:, kc, n * 128:(n + 1) * 128], wpool.ap[:, sw, kc, 0:32], kc == 0, kc == KC - 1,
                   (hTk[n], wslot[sw]), (bk.b,))
            cp(bgs.ap[:, n, :], bk.ap[:, 0:32], (bk.b,), (bgs.b,), eng="act")
        act(beta.ap, bgs.ap[:, :, 0:16], AF.Sigmoid, (bgs.b,), (beta.b,))
        tt(t16.ap, bgs.ap[:, :, 16:32], bcm(sm.ap[:, C_DTB:C_DTB + 16], NB), ALU.add, (bgs.b, sm.b), (t16.b,))
        act(t16.ap, t16.ap, AF.Exp, (t16.b,), (t16.b,))
        act(t16.ap, t16.ap, AF.Ln, (t16.b,), (t16.b,), bias=1.0)
        tt(gg.ap, t16.ap, bcm(negA.ap, NB), ALU.mult, (t16.b, negA.b), (gg.b,))
        cp(ghl.ap[:, :, 0:16], gg.ap, (gg.b,), (ghl.b,))
        tt(ghl.ap[:, :, 16:32], gg.ap, ghl.ap[:, :, 0:16], ALU.subtract, (gg.b, ghl.b), (ghl.b,))
        for n in range(NB):
            bk = small()
            for q, lt in enumerate((uleb, ugtb, onesb)):
                mm(bk.ap[:, q * 16:(q + 1) * 16], lt, ghl.ap[:, n, 0:16], True, False, (cb.b, ghl.b), (bk.b,))
                mm(bk.ap[:, q * 16:(q + 1) * 16], lt, ghl.ap[:, n, 16:32], False, True, (cb.b, ghl.b), (bk.b,))
            act(egkl.ap[:, n, :], bk.ap[:, 0:48], AF.Exp, (bk.b,), (egkl.b,))

        pre = [wk.get([128, TB + 3], F32, f"pre{i}") for i in range(2)]
        cacc = [wk.get([128, TB], F32, f"cacc{i}") for i in range(2)]
        s1 = [[wk.get([128, TB], BF, f"s1_{par}_{i}") for i in range(4)] for par in range(4)]
        sqs = wk.get([128, NB, 128], F32, "sqs")
        ojunk = wk.get([128, 128], BF, "ojunk")

        def nsl(n):
            return slice(n * 128, (n + 1) * 128)

        def stage1(h):
            par = h % 4
            steps = []

            def mk(idx, off):
                st = {}
                tix = off // 128 + h

                def post():
                    bk = st["bk"]
                    if idx < 3:
                        pr = pre[idx % 2]
                        cp(pr.ap[:, 0:3], tails_qkv.ap[:, tix, :], (tails_qkv.b,), (pr.b,))
                        cp(pr.ap[:, 3:3 + TB], bk.ap[:, 0:TB], (bk.b,), (pr.b,), eng="act")
                        cp(tails_qkv.ap[:, tix, :], pr.ap[:, TB:TB + 3], (pr.b,), (tails_qkv.b,))
                        ac = cacc[idx % 2]
                        wc = C_WCONV + tix * 4
                        ts(ac.ap, pr.ap[:, 0:TB], smc(wc), ALU.mult, (pr.b, sm.b), (ac.b,))
                        for k in range(1, 4):
                            stt(ac.ap, pr.ap[:, k:k + TB], smc(wc + k), ac.ap, ALU.mult, ALU.add, (pr.b, sm.b, ac.b), (ac.b,))
                        act(s1[par][idx].ap, ac.ap, AF.Silu, (ac.b,), (s1[par][idx].b,))
                    else:
                        act(s1[par][3].ap, bk.ap[:, 0:TB], AF.Silu, (bk.b,), (s1[par][3].b,))

                def grp(k0, k1):
                    def f():
                        if k0 == 0:
                            st["s"] = pre_slots[h][idx]
                            st["bk"] = big()
                        s_, bk = st["s"], st["bk"]
                        for kc in range(k0, k1):
                            mm(bk.ap[:, 0:TB], wpool.ap[:, s_, kc, :], hT.ap[:, kc, :], kc == 0, kc == KC - 1,
                               HT + (wslot[s_],), (bk.b,))
                        if k1 == KC:
                            post()
                            s1_done[h] = s1_done.get(h, 0) + 1
                    return f

                return [grp(k, k + MMG) for k in range(0, KC, MMG)]

            for idx, off in enumerate((OFF_Q, OFF_K, OFF_V, OFF_ZA)):
                steps += mk(idx, off)
            return steps

        def make_env(tag):
            ks_tok = wk.get([128, NB, 128], BF, "ks_tok")
            qs_tok = wk.get([128, NB, 128], BF, "qs_tok")
            vb = wk.get([128, NB, 128], BF, "vb")
            ssk = wk.get([128, NB], F32, "ssk")
            ssq = wk.get([128, NB], F32, "ssq")
            ltmp = wk.get([128, NB], F32, "ltmp")
            rs_k = wk.get([128, NB], F32, "rs_k")
            rs_q = wk.get([128, NB], F32, "rs_q")
            c_t = wk.get([128, NB], F32, "c_t")
            c_kbg = wk.get([128, NB], F32, "c_kbg")
            c_ke = wk.get([128, NB], F32, "c_ke")
            c_qd = wk.get([128, NB], F32, "c_qd")
            Dk = wk.get([128, NB, 128], BF, "Dk")
            Dq = wk.get([128, NB, 128], BF, "Dq")
            Dqd = wk.get([128, NB, 128], BF, "Dqd")
            Do = Dk
            knT = wk.get([128, TB], BF, "knT")
            qnT = wk.get([128, TB], BF, "qnT")
            qdT = wk.get([128, TB], BF, "qdT")
            kbg = wk.get([128, NB, 128], BF, "kbg")
            ke = wk.get([128, NB, 128], BF, "ke")
            Rm = wk.get([128, NB, 128], BF, "Rm")
            SBm = wk.get([128, NB, 128], BF, "SBm")
            Mfull = wk.get([128, NB, 128], BF, "Mfull")
            Mb = wk.get([128, NB, 128], BF, "Mb")
            Am = wk.get([128, NB, 128], BF, "Am")
            Bm = wk.get([128, NB, 128], BF, "Bm")
            Tm = wk.get([128, NB, 128], BF, "Tm")
            Xm = wk.get([128, NB, 128], BF, "Xm")
            Yn = wk.get([128, NB, 128], BF, "Yn")
            Zn = wk.get([128, NB, 128], BF, "Zn")
            Eb = [wk.get([128, NB, 128], BF, "Eb0")]
            Fb = [wk.get([128, NB, 128], BF, "Fb0")]
            QKm = Rm
            QKmT = wk.get([128, NB, 128], BF, "QKmT")
            nwT = qs_tok
            o_tok = ks_tok
            vn = [wk.get([128, 128], BF, f"vn{i}") for i in range(2)]
            S_bf = wk.get([128, 128], BF, "S_bf")
            sso = wk.get([128, NB], F32, "sso")
            rso = wk.get([128, NB], F32, "rso")

            def stage2(h):
                par = h % 4
                qsT, ksT, vsT, zsT = s1[par]
                steps = []
                beta_h = beta.ap[:, :, h]
                eg_h = egkl.ap[:, :, h]
                ek_h = egkl.ap[:, :, 16 + h]

                def sa():
                    for src, dst, ssx in ((ksT, ks_tok, ssk), (qsT, qs_tok, ssq)):
                        bk = small()
                        for n in range(NB):
                            mm(bk.ap[:, nsl(n)], src.ap[:, nsl(n)], identb, True, True, (src.b, cb.b), (bk.b,))
                        cp(dst.ap, bk.ap[:, 0:TB].rearrange("p (a b) -> p a b", a=NB), (bk.b,), (dst.b,), eng="act")
                        tt(sqs.ap, dst.ap, dst.ap, ALU.mult, (dst.b,), (sqs.b,))
                        P.op("dve", lambda v, o=ssx.ap, i=sqs.ap: v.tensor_reduce(out=o, in_=i, op=ALU.add, axis=AX.X),
                             (sqs.b,), (ssx.b,))
                    bk = small()
                    for n in range(NB):
                        mm(bk.ap[:, nsl(n)], vsT.ap[:, nsl(n)], identb, True, True, (vsT.b, cb.b), (bk.b,))
                    tt(vb.ap, bk.ap[:, 0:TB].rearrange("p (a b) -> p a b", a=NB), bc(beta_h, 128), ALU.mult,
                       (bk.b, beta.b), (vb.b,))

                def sb():
                    rsqrt_small(rs_k.ap, ssk.ap, (ssk.b,), (rs_k.b,), ltmp)
                    rsqrt_small(rs_q.ap, ssq.ap, (ssq.b,), (rs_q.b,), ltmp, post_bias=-0.5 * math.log(128.0))
                    tt(c_t.ap, rs_k.ap, beta_h, ALU.mult, (rs_k.b, beta.b), (c_t.b,))
                    tt(c_kbg.ap, c_t.ap, eg_h, ALU.mult, (c_t.b, egkl.b), (c_kbg.b,))
                    tt(c_ke.ap, rs_k.ap, ek_h, ALU.mult, (rs_k.b, egkl.b), (c_ke.b,))
                    tt(c_qd.ap, rs_q.ap, eg_h, ALU.mult, (rs_q.b, egkl.b), (c_qd.b,))
                    idb = bcm(identb, NB)
                    tt(Dk.ap, idb, bc(rs_k.ap, 128), ALU.mult, (cb.b, rs_k.b), (Dk.b,))
                    tt(Dq.ap, idb, bc(rs_q.ap, 128), ALU.mult, (cb.b, rs_q.b), (Dq.b,))
                    tt(Dqd.ap, idb, bc(c_qd.ap, 128), ALU.mult, (cb.b, c_qd.b), (Dqd.b,))
                    tt(kbg.ap, ks_tok.ap, bc(c_kbg.ap, 128), ALU.mult, (ks_tok.b, c_kbg.b), (kbg.b,))
                    tt(ke.ap, ks_tok.ap, bc(c_ke.ap, 128), ALU.mult, (ks_tok.b, c_ke.b), (ke.b,))

                    tt(Rm.ap, bcm(ugtb, NB), bc(ghl.ap[:, :, h], 128), ALU.mult, (cb.b, ghl.b), (Rm.b,))
                    tt(SBm.ap, strict4.ap, bc(beta_h, 128), ALU.mult, (strict4.b, beta.b), (SBm.b,))

                def sc():
                    for src, dg, dst in ((ks_tok, Dk, knT), (qs_tok, Dq, qnT), (qs_tok, Dqd, qdT)):
                        bk = small()
                        for n in range(NB):
                            mm(bk.ap[:, nsl(n)], src.ap[:, n, :], dg.ap[:, n, :], True, True, (src.b, dg.b), (bk.b,))
                        cp(dst.ap, bk.ap[:, 0:TB], (bk.b,), (dst.b,), eng="act")
                    bk = small()
                    mm(bk.ap[:, 0:TB], uleb, Rm.ap.rearrange("p a b -> p (a b)"), True, False, (cb.b, Rm.b), (bk.b,))
                    mm(bk.ap[:, 0:TB], identb, negm4.ap.rearrange("p a b -> p (a b)"), False, True, (cb.b, negm4.b), (bk.b,))
                    act(Mfull.ap, bk.ap[:, 0:TB].rearrange("p (a b) -> p a b", a=NB), AF.Exp, (bk.b,), (Mfull.b,))
                    tt(Mb.ap, Mfull.ap, SBm.ap, ALU.mult, (Mfull.b, SBm.b), (Mb.b,))

                def sd():
                    bk = small()
                    for n in range(NB):
                        mm(bk.ap[:, nsl(n)], knT.ap[:, nsl(n)], knT.ap[:, nsl(n)], True, True, (knT.b,), (bk.b,))
                    tt(Am.ap, bk.ap[:, 0:TB].rearrange("p (a b) -> p a b", a=NB), Mb.ap, ALU.mult, (bk.b, Mb.b), (Am.b,))
                    bk = small()
                    for n in range(NB):
                        mm(bk.ap[:, nsl(n)], qnT.ap[:, nsl(n)], knT.ap[:, nsl(n)], True, True, (qnT.b, knT.b), (bk.b,))
                    tt(QKm.ap, bk.ap[:, 0:TB].rearrange("p (a b) -> p a b", a=NB), Mfull.ap, ALU.mult, (bk.b, Mfull.b), (QKm.b,))

                def se():
                    bk = small()
                    for n in range(NB):
                        mm(bk.ap[:, nsl(n)], Am.ap[:, n, :], identb, True, True, (Am.b, cb.b), (bk.b,))
                    cp(Bm.ap, bk.ap[:, 0:TB].rearrange("p (a b) -> p a b", a=NB), (bk.b,), (Bm.b,), eng="act")
                    bk = small()
                    for n in range(NB):
                        mm(bk.ap[:, nsl(n)], QKm.ap[:, n, :], identb, True, True, (QKm.b, cb.b), (bk.b,))
                    cp(QKmT.ap, bk.ap[:, 0:TB].rearrange("p (a b) -> p a b", a=NB), (bk.b,), (QKmT.b,), eng="act")

                def st0():
                    tt(Eb[0].ap, Am.ap, bcm(cb.ap[:, 6, :], NB), ALU.mult, (Am.b, cb.b), (Eb[0].b,), eng="pool")
                    tt(Fb[0].ap, Bm.ap, bcm(cb.ap[:, 13, :], NB), ALU.mult, (Bm.b, cb.b), (Fb[0].b,), eng="pool")
                    tt(Tm.ap, bcm(identb, NB), Eb[0].ap, ALU.subtract, (cb.b, Eb[0].b), (Tm.b,))
                    tt(Xm.ap, bcm(identb, NB), Fb[0].ap, ALU.subtract, (cb.b, Fb[0].b), (Xm.b,))

                def mkstage(sg):
                    def f():
                        E = Eb[sg % len(Eb)]
                        F = Fb[sg % len(Fb)]
                        last = sg == 6
                        v3 = lambda bk: bk.ap[:, 0:TB].rearrange("p (a b) -> p a b", a=NB)
                        tt(E.ap, Am.ap, bcm(cb.ap[:, 6 + sg, :], NB), ALU.mult, (Am.b, cb.b), (E.b,), eng="pool")
                        if not last:
                            tt(F.ap, Bm.ap, bcm(cb.ap[:, 13 + sg, :], NB), ALU.mult, (Bm.b, cb.b), (F.b,), eng="pool")
                        bY = small()
                        for n in range(NB):
                            mm(bY.ap[:, nsl(n)], E.ap[:, n, :], Xm.ap[:, n, :], True, True, (E.b, Xm.b), (bY.b,))
                        act(Yn.ap, v3(bY), AF.Identity, (bY.b,), (Yn.b,), scale=-1.0)
                        if not last:
                            bZ = small()
                            for n in range(NB):
                                mm(bZ.ap[:, nsl(n)], F.ap[:, n, :], Tm.ap[:, n, :], True, True, (F.b, Tm.b), (bZ.b,))
                            act(Zn.ap, v3(bZ), AF.Identity, (bZ.b,), (Zn.b,), scale=-1.0)
                        bX = small()
                        for n in range(NB):
                            mm(bX.ap[:, nsl(n)], identb, Xm.ap[:, n, :], True, False, (cb.b, Xm.b), (bX.b,))
                            mm(bX.ap[:, nsl(n)], Tm.ap[:, n, :], Yn.ap[:, n, :], False, True, (Tm.b, Yn.b), (bX.b,))
                        if not last:
                            bT = small()
                            for n in range(NB):
                                mm(bT.ap[:, nsl(n)], identb, Tm.ap[:, n, :], True, False, (cb.b, Tm.b), (bT.b,))
                                mm(bT.ap[:, nsl(n)], Xm.ap[:, n, :], Zn.ap[:, n, :], False, True, (Xm.b, Zn.b), (bT.b,))
                        cp(Xm.ap, v3(bX), (bX.b,), (Xm.b,))
                        if not last:
                            cp(Tm.ap, v3(bT), (bT.b,), (Tm.b,))
                    return f

                def sw_():
                    bk = small()
                    for n in range(NB):
                        mm(bk.ap[:, nsl(n)], kbg.ap[:, n, :], Xm.ap[:, n, :], True, True, (kbg.b, Xm.b), (bk.b,))
                    ts(nwT.ap, bk.ap[:, 0:TB].rearrange("p (a b) -> p a b", a=NB), -1.0, ALU.mult, (bk.b,), (nwT.b,))
                    cp(S_bf.ap, Sst.ap[:, h, :], (Sst_h[h],), (S_bf.b,), eng="act")

                def mkrec(n):
                    def f():
                        bk = small()
                        v = vn[n % 2]
                        mm(bk.ap[:, 0:128], Xm.ap[:, n, :], vb.ap[:, n, :], True, False, (Xm.b, vb.b), (bk.b,))
                        mm(bk.ap[:, 0:128], nwT.ap[:, n, :], S_bf.ap, False, True, (nwT.b, S_bf.b), (bk.b,))
                        cp(v.ap, bk.ap[:, 0:128], (bk.b,), (v.b,), eng="act")
                        mm(bk.ap[:, 128:256], ke.ap[:, n, :], v.ap, True, True, (ke.b, v.b), (bk.b,))
                        mm(bk.ap[:, 256:384], qdT.ap[:, nsl(n)], S_bf.ap, True, False, (qdT.b, S_bf.b), (bk.b,))
                        mm(bk.ap[:, 256:384], QKmT.ap[:, n, :], v.ap, False, True, (QKmT.b, v.b), (bk.b,))
                        stt(Sst.ap[:, h, :], Sst.ap[:, h, :], egkl.ap[:, n, 32 + h:33 + h], bk.ap[:, 128:256], ALU.mult, ALU.add,
                            (Sst_h[h], egkl.b, bk.b), (Sst_h[h],))
                        cp(S_bf.ap, Sst.ap[:, h, :], (Sst_h[h],), (S_bf.b,), eng="act")
                        cp(o_tok.ap[:, n, :], bk.ap[:, 256:384], (bk.b,), (o_tok.b,))
                        act(ojunk.ap, bk.ap[:, 256:384], AF.Square, (bk.b,), (ojunk.b, sso.b), accum_out=sso.ap[:, n:n + 1])
                    return f

                def sp_():
                    rsqrt_small(rso.ap, sso.ap, (sso.b,), (rso.b,), ltmp, scale=1.0 / 128)
                    tt(Do.ap, bcm(identb, NB), bc(rso.ap, 128), ALU.mult, (cb.b, rso.b), (Do.b,))
                    bk = small()
                    for n in range(NB):
                        mm(bk.ap[:, nsl(n)], o_tok.ap[:, n, :], Do.ap[:, n, :], True, True, (o_tok.b, Do.b), (bk.b,))
                    stt(oaT.ap[:, h, :], bk.ap[:, 0:TB], smc(C_GDN), zsT.ap, ALU.mult, ALU.mult, (bk.b, sm.b, zsT.b), (oaT.b,))

                steps += [sa, sb, sc, sd, se, st0]
                steps += [mkstage(sg) for sg in range(1, 7)]
                steps += [sw_]
                steps += [mkrec(n) for n in range(NB)]
                steps += [sp_]
                return steps

            return stage2

        envs = [make_env(0), make_env(1)]

        def interleave(a, b):
            out = []
            la, lb = len(a), len(b)
            if la == 0:
                return list(b)
            pos = [int((i + 0.5) * lb / la) for i in range(la)]
            ai = 0
            for j in range(lb + 1):
                while ai < la and pos[ai] == j:
                    out.append(a[ai])
                    ai += 1
                if j < lb:
                    out.append(b[j])
            return out

        pre_slots = {}

        s1_done = {}

        def prefetch(hh):
            if hh < H:
                assert hh < 2 or s1_done.get(hh - 2, 0) == 4, ("prefetch before stage1 done", hh)
                pre_slots[hh] = [load_w(w_in_d[:, off + hh * 128:off + (hh + 1) * 128])
                                 for off in (OFF_Q, OFF_K, OFF_V, OFF_ZA)]

        def zipsteps(a, b):
            out = []
            for i in range(max(len(a), len(b))):
                if i < len(a):
                    out.append(a[i])
                if i < len(b):
                    out.append(b[i])
            return out

        prefetch(0)
        prefetch(1)
        for f in stage1(0) + stage1(1):
            f()
        seqs = [[], []]
        for hh in range(H):
            st = envs[hh % 2](hh)
            seqs[hh % 2] += [(hh, i, f) for i, f in enumerate(st)]
        nst = len(seqs[0]) // (H // 2)
        off = nst // 2 if KOFF < 0 else KOFF
        merged = []
        for i in range(len(seqs[0]) + off):
            if i < len(seqs[0]):
                merged.append(seqs[0][i])
            j = i - off
            if 0 <= j < len(seqs[1]):
                merged.append(seqs[1][j])
        startpos = {}
        for pos, (hh, i, f) in enumerate(merged):
            if i == 0:
                startpos[hh] = pos
        inserts = {}
        for hh in range(2, H):
            fl = stage1(hh)
            w1 = startpos[hh] - 4
            w0 = max(0, w1 - KWIN)
            for q_, f in enumerate(fl):
                pos = w0 + int(q_ * (w1 - w0) / len(fl))
                inserts.setdefault(pos, []).append(f)
            ppos = max(0, w0 - nst // 2)
            inserts.setdefault(ppos, []).insert(0, (lambda hx=hh: prefetch(hx)))
        done_pf = set()
        for pos, (hh, i, f) in enumerate(merged):
            for g in inserts.get(pos, []):
                g()
            f()

        if dbg and blk == NBLK - 1:
            P.barrier()
            dt_ = wk.get([128, 16, TB], F32, "dbgt")
            cp(dt_.ap, oaT.ap, (oaT.b,), (dt_.b,))
            dma("sp", dbg_d[:, 0, :, :], dt_.ap, (dt_.b,), (), "dbg")

        if KSTOP < 2:
            continue
        P.barrier()
        wk.reset()
        DW = [wk.get([128, CK, 128], BF, f"DW{i}") for i in range(2)]
        upre = [wk.get([128, TB + 30], BF, f"upre{i}") for i in range(2)]
        sgB = wk.get([128, TB], F32, "sgB")
        usq = [wk.get([128, TB], BF, f"usq{i}") for i in range(2)]
        acc1 = wk.get([128, TB], F32, "acc1")
        acc2 = wk.get([128, TB], F32, "acc2")
        meanB = wk.get([128, TB], F32, "meanB")
        varB = wk.get([128, TB], F32, "varB")
        rstdB = wk.get([128, TB], F32, "rstdB")
        nmrB = wk.get([128, TB], F32, "nmrB")
        szb = wk.get([128, TB], BF, "szb")
        t1 = wk.get([128, TB], F32, "t1")
        t3 = wk.get([128, TB], BF, "t3")
        ubk = [Buf(f"ub{c}") for c in range(KC)]
        memset(acc1.ap, 0.0, (acc1.b,))
        memset(acc2.ap, 0.0, (acc2.b,))
        def b_inproj(ct):
            up, dw = upre[ct % 2], DW[ct % 2]
            sa_ = load_w(w_in_d[:, OFF_GLU + ct * 128:OFF_GLU + (ct + 1) * 128])
            sb_ = load_w(w_in_d[:, OFF_GLU + D + ct * 128:OFF_GLU + D + (ct + 1) * 128])
            tt(dw.ap, bcm(identb, CK), bc(sm.ap[:, C_WDW + ct * CK:C_WDW + (ct + 1) * CK], 128), ALU.mult,
               (cb.b, sm.b), (dw.b,), eng="pool")
            bA = big()
            for kc in range(KC):
                mm(bA.ap[:, 0:TB], wpool.ap[:, sa_, kc, :], hT.ap[:, kc, :], kc == 0, kc == KC - 1, HT + (wslot[sa_],), (bA.b,))
            bB = big()
            for kc in range(KC):
                mm(bB.ap[:, 0:TB], wpool.ap[:, sb_, kc, :], hT.ap[:, kc, :], kc == 0, kc == KC - 1, HT + (wslot[sb_],), (bB.b,))
            act(sgB.ap, bB.ap[:, 0:TB], AF.Sigmoid, (bB.b,), (sgB.b,))
            cp(up.ap[:, 0:30], tails_u.ap[:, ct, :], (tails_u.b,), (up.b,))
            tt(up.ap[:, 30:30 + TB], bA.ap[:, 0:TB], sgB.ap, ALU.mult, (bA.b, sgB.b), (up.b,))
            cp(tails_u.ap[:, ct, :], up.ap[:, TB:TB + 30], (up.b,), (tails_u.b,))

        def b_conv(ct):
            up, dw, uq = upre[ct % 2], DW[ct % 2], usq[ct % 2]
            bC = big()
            for k in range(CK):
                mm(bC.ap[:, 0:TB], dw.ap[:, k, :], up.ap[:, k:k + TB], k == 0, k == CK - 1, (dw.b, up.b), (bC.b,))
            act(ubT.ap[:, ct, :], bC.ap[:, 0:TB], AF.Identity, (bC.b, sm.b), (ubk[ct],), bias=smc(C_BDW + ct))
            act(uq.ap, bC.ap[:, 0:TB], AF.Square, (bC.b, sm.b), (uq.b,), bias=smc(C_BDW + ct))

        def b_stats(ct):
            uq = usq[ct % 2]
            bS = small()
            mm(bS.ap[:, 0:TB], onesb, ubT.ap[:, ct, :], True, True, (cb.b, ubk[ct]), (bS.b,))
            tt(acc1.ap, acc1.ap, bS.ap[:, 0:TB], ALU.add, (acc1.b, bS.b), (acc1.b,))
            bS2 = small()
            mm(bS2.ap[:, 0:TB], onesb, uq.ap, True, True, (cb.b, uq.b), (bS2.b,))
            tt(acc2.ap, acc2.ap, bS2.ap[:, 0:TB], ALU.add, (acc2.b, bS2.b), (acc2.b,))

        b_inproj(0)
        for ct in range(KC):
            if ct + 1 < KC:
                b_inproj(ct + 1)
            b_conv(ct)
            if ct > 0:
                b_stats(ct - 1)
        b_stats(KC - 1)
        ts(meanB.ap, acc1.ap, 1.0 / D, ALU.mult, (acc1.b,), (meanB.b,))
        tt(varB.ap, meanB.ap, meanB.ap, ALU.mult, (meanB.b,), (varB.b,))
        stt(varB.ap, acc2.ap, 1.0 / D, varB.ap, ALU.mult, ALU.subtract, (acc2.b, varB.b), (varB.b,))
        act(t1.ap, varB.ap, AF.Ln, (varB.b,), (t1.b,), bias=EPS)
        act(rstdB.ap, t1.ap, AF.Exp, (t1.b,), (rstdB.b,), scale=-0.5)
        stt(nmrB.ap, meanB.ap, -1.0, rstdB.ap, ALU.mult, ALU.mult, (meanB.b, rstdB.b), (nmrB.b,))
        for ct in range(KC):
            sz = load_w(w_in_d[:, OFF_ZB + ct * 128:OFF_ZB + (ct + 1) * 128])
            bZ = big()
            for kc in range(KC):
                mm(bZ.ap[:, 0:TB], wpool.ap[:, sz, kc, :], hT.ap[:, kc, :], kc == 0, kc == KC - 1, HT + (wslot[sz],), (bZ.b,))
            act(szb.ap, bZ.ap[:, 0:TB], AF.Silu, (bZ.b,), (szb.b,))
            tt(t1.ap, ubT.ap[:, ct, :], rstdB.ap, ALU.mult, (ubk[ct], rstdB.b), (t1.b,))
            tt(t1.ap, t1.ap, nmrB.ap, ALU.add, (t1.b, nmrB.b), (t1.b,))
            act(t3.ap, t1.ap, AF.Silu, (t1.b, sm.b), (t3.b,), scale=smc(C_LNG + ct), bias=smc(C_LNB + ct))
            tt(ubT.ap[:, ct, :], t3.ap, szb.ap, ALU.mult, (t3.b, szb.b), (ubk[ct],))
        UB = tuple(ubk)

        if dbg and blk == NBLK - 1:
            P.barrier()
            dt_ = wk.get([128, 16, TB], F32, "dbgt")
            cp(dt_.ap, ubT.ap, UB, (dt_.b,))
            dma("sp", dbg_d[:, 1, :, :], dt_.ap, (dt_.b,), (), "dbg")

        if KSTOP < 3:
            continue
        P.barrier()
        wk.reset()
        gAf = wk.get([128, TB], F32, "gAf")
        gBf = wk.get([128, TB], F32, "gBf")
        m1 = wk.get([128, TB], F32, "m1")
        m2 = wk.get([128, TB], F32, "m2")
        mik = [Buf(f"mi{c}") for c in range(KC)]
        for dtile in range(KC):
            cs = slice(dtile * 128, (dtile + 1) * 128)
            s_a = load_w(w_bra_d[:, cs])
            s_ga = load_w(w_in_d[:, OFF_GATE + dtile * 128:OFF_GATE + (dtile + 1) * 128])
            s_b = load_w(w_brb_d[:, cs])
            s_gb = load_w(w_in_d[:, OFF_GATE + D + dtile * 128:OFF_GATE + D + (dtile + 1) * 128])
            b1 = big()
            for kc in range(KC):
                mm(b1.ap[:, 0:TB], wpool.ap[:, s_a, kc, :], oaT.ap[:, kc, :], kc == 0, kc == KC - 1, (oaT.b, wslot[s_a]), (b1.b,))
            b2 = big()
            for kc in range(KC):
                mm(b2.ap[:, 0:TB], wpool.ap[:, s_ga, kc, :], hT.ap[:, kc, :], kc == 0, kc == KC - 1, HT + (wslot[s_ga],), (b2.b,))
            act(gAf.ap, b2.ap[:, 0:TB], AF.Sigmoid, (b2.b, sm.b), (gAf.b,), bias=smc(C_BGATE + dtile))
            tt(m1.ap, b1.ap[:, 0:TB], gAf.ap, ALU.mult, (b1.b, gAf.b), (m1.b,))
            b3 = big()
            for kc in range(KC):
                mm(b3.ap[:, 0:TB], wpool.ap[:, s_b, kc, :], ubT.ap[:, kc, :], kc == 0, kc == KC - 1, UB + (wslot[s_b],), (b3.b,))
            b4 = big()
            for kc in range(KC):
                mm(b4.ap[:, 0:TB], wpool.ap[:, s_gb, kc, :], hT.ap[:, kc, :], kc == 0, kc == KC - 1, HT + (wslot[s_gb],), (b4.b,))
            act(gBf.ap, b4.ap[:, 0:TB], AF.Sigmoid, (b4.b, sm.b), (gBf.b,), bias=smc(C_BGATE + 16 + dtile))
            tt(m2.ap, b3.ap[:, 0:TB], gBf.ap, ALU.mult, (b3.b, gBf.b), (m2.b,))
            tt(minT.ap[:, dtile, :], m1.ap, m2.ap, ALU.add, (m1.b, m2.b), (mik[dtile],))
        MI = tuple(mik)

        if dbg and blk == NBLK - 1:
            P.barrier()
            dt_ = wk.get([128, 16, TB], F32, "dbgt")
            cp(dt_.ap, minT.ap, MI, (dt_.b,))
            dma("sp", dbg_d[:, 2, :, :], dt_.ap, (dt_.b,), (), "dbg")

        if KSTOP < 4:
            continue
        P.barrier()
        wk.reset()
        xt = [wk.get([128, D], F32, f"xtd{i}") for i in range(2)]
        grow = wk.get([128, D], F32, "grow")
        tmpD = wk.get([128, D], F32, "tmpD")
        junkD = wk.get([128, D], BF, "junkD")
        ss1 = wk.get([128, NB], F32, "ss1")
        l1 = wk.get([128, NB], F32, "l1")
        rstd1 = wk.get([128, NB], F32, "rstd1")
        mxk = [Buf(f"mx{n}") for n in range(NB)]
        dma("sp", grow.ap, rows_d[:, 0:D], (), (grow.b,), "grow")
        for cg in range(4):
            s0 = load_wide(w_out_d[:, cg * 512:(cg + 1) * 512])
            for n in range(NB):
                bk = big()
                for kc in range(KC):
                    mm(bk.ap[:, 0:512], minT.ap[:, kc, nsl(n)], wpool.ap[:, s0:s0 + 4, kc, :], kc == 0, kc == KC - 1,
                       MI + tuple(wslot[s0:s0 + 4]), (bk.b,))
                cp(mixed.ap[:, n, cg * 512:(cg + 1) * 512], bk.ap[:, 0:512], (bk.b,), (mxk[n],),
                   eng=("act" if (n + cg) % 2 else "dve"))
        for n in range(NB):
            xs = xt[n % 2]
            dma("sp", xs.ap, x_d[t0 + n * 128:t0 + (n + 1) * 128, :], (), (xs.b,), f"xt{n % 2}")
            act(junkD.ap, mixed.ap[:, n, :], AF.Square, (mxk[n],), (junkD.b, ss1.b), accum_out=ss1.ap[:, n:n + 1])
            act(l1.ap[:, n:n + 1], ss1.ap[:, n:n + 1], AF.Ln, (ss1.b,), (l1.b,), scale=1.0 / D, bias=EPS)
            act(rstd1.ap[:, n:n + 1], l1.ap[:, n:n + 1], AF.Exp, (l1.b,), (rstd1.b,), scale=-0.5)
            stt(tmpD.ap, mixed.ap[:, n, :], rstd1.ap[:, n:n + 1], grow.ap, ALU.mult, ALU.mult, (mxk[n], rstd1.b, grow.b), (tmpD.b,))
            tt(mixed.ap[:, n, :], tmpD.ap, xs.ap, ALU.add, (tmpD.b, xs.b), (mxk[n],))

        if KSTOP < 5:
            continue
        P.barrier()
        wk.reset()
        grow = wk.get([128, D], F32, "growE")
        x1b = wk.get([128, D], BF, "x1b")
        ptile = wk.get([128, PLE], F32, "ptile")
        pbt = wk.get([128, PLE], BF, "pbt")
        pT = wk.get([128, 2, TB], BF, "pT")
        wpp = wk.get([128, 2, D], BF, "wpp")
        sgE = wk.get([128, 512], F32, "sgE")
        tmpE = [wk.get([128, D], F32, f"tmpE{i}") for i in range(2)]
        junkE = x1b
        vbuf = wk.get([128, NB, D], F32, "vbuf")
        ss2 = wk.get([128, NB], F32, "ss2")
        l2 = wk.get([128, NB], F32, "l2")
        rstd2 = wk.get([128, NB], F32, "rstd2")
        x1k = [Buf(f"x1T{n}") for n in range(NB)]
        vbk = [Buf(f"vb{n}") for n in range(NB)]
        dma("sp", grow.ap, rows_d[:, D:2 * D], (), (grow.b,), "grow")
        for j in range(4):
            dma("pool", wpp.ap[:, :, j * 512:(j + 1) * 512],
                w_pp_d[:, j * 512:(j + 1) * 512].rearrange("(kc p) c -> p kc c", p=128), (), (wpp.b,), "wpp")
        for n in range(NB):
            cp(x1b.ap, mixed.ap[:, n, :], (mxk[n],), (x1b.b,), eng="act")
            for g in range(4):
                bk = big()
                for j in range(4):
                    kc = g * 4 + j
                    mm(bk.ap[:, j * 128:(j + 1) * 128], x1b.ap[:, kc * 128:(kc + 1) * 128], identb, True, True, (x1b.b, cb.b), (bk.b,))
                cp(x1T.ap[:, g * 4:g * 4 + 4, nsl(n)], bk.ap.rearrange("p (a b) -> p a b", a=4), (bk.b,), (x1k[n],),
                   eng=("act" if g % 2 else "dve"))
            dma("sp", ptile.ap, p_d[t0 + n * 128:t0 + (n + 1) * 128, :], (), (ptile.b,), "ptile")
            cp(pbt.ap, ptile.ap, (ptile.b,), (pbt.b,))
            bk = small()
            for j in range(2):
                mm(bk.ap[:, j * 128:(j + 1) * 128], pbt.ap[:, j * 128:(j + 1) * 128], identb, True, True, (pbt.b, cb.b), (bk.b,))
            cp(pT.ap[:, :, nsl(n)], bk.ap[:, 0:256].rearrange("p (a b) -> p a b", a=2), (bk.b,), (pT.b,), eng="act")
        for cg in range(4):
            s0 = load_wide(w_pg_d[:, cg * 512:(cg + 1) * 512])
            for n in range(NB):
                bG = big()
                for kc in range(KC):
                    mm(bG.ap[:, 0:512], x1T.ap[:, kc, nsl(n)], wpool.ap[:, s0:s0 + 4, kc, :], kc == 0, kc == KC - 1,
                       (x1k[n],) + tuple(wslot[s0:s0 + 4]), (bG.b,))
                bE = big()
                for kc in range(2):
                    mm(bE.ap[:, 0:512], pT.ap[:, kc, nsl(n)], wpp.ap[:, kc, cg * 512:(cg + 1) * 512], kc == 0, kc == 1,
                       (pT.b, wpp.b), (bE.b,))
                act(sgE.ap, bG.ap[:, 0:512], AF.Sigmoid, (bG.b,), (sgE.b,))
                tt(vbuf.ap[:, n, cg * 512:(cg + 1) * 512], bE.ap[:, 0:512], sgE.ap, ALU.mult, (bE.b, sgE.b), (vbk[n],))
        for n in range(NB):
            tE = tmpE[n % 2]
            act(junkE.ap, vbuf.ap[:, n, :], AF.Square, (vbk[n],), (junkE.b, ss2.b), accum_out=ss2.ap[:, n:n + 1])
            act(l2.ap[:, n:n + 1], ss2.ap[:, n:n + 1], AF.Ln, (ss2.b,), (l2.b,), scale=1.0 / D, bias=EPS)
            act(rstd2.ap[:, n:n + 1], l2.ap[:, n:n + 1], AF.Exp, (l2.b,), (rstd2.b,), scale=-0.5)
            stt(tE.ap, vbuf.ap[:, n, :], rstd2.ap[:, n:n + 1], grow.ap, ALU.mult, ALU.mult, (vbk[n], rstd2.b, grow.b), (tE.b,))
            tt(tE.ap, tE.ap, mixed.ap[:, n, :], ALU.add, (tE.b, mxk[n]), (tE.b,))
            dma("sp", out_d[t0 + n * 128:t0 + (n + 1) * 128, :], tE.ap, (tE.b,), (), f"st{n % 2}")
        P.barrier()

    P.barrier()
    P.finalize()
    esem = {n: es.enter_context(nc.semaphore(f"e_{n}")) for n in Prog.ENGS}
    dsem = {k: es.enter_context(nc.semaphore(f"d_{k}")) for k in P.dcount}
    block = es.enter_context(nc.Block())

    @block.tensor
    def _(t):
        P.replay("pe", t, esem, dsem)

    @block.scalar
    def _(s):
        P.replay("act", s, esem, dsem)

    @block.vector
    def _(v):
        P.replay("dve", v, esem, dsem)

    @block.gpsimd
    def _(g):
        P.replay("pool", g, esem, dsem)

    @block.sync
    def _(sy):
        P.replay("sp", sy, esem, dsem)

    es.close()
    return nc


def host_consts(g_pre, b_gate, w_conv_qkv, a_log, dt_bias, g_dn_out, w_dw, b_dw, ln_g, ln_b, g_post, g_ple):
    sm = np.zeros((128, NS), np.float32)
    sm[:, C_GPRE:C_GPRE + 16] = g_pre[0].reshape(16, 128).T
    sm[:, C_BGATE:C_BGATE + 32] = b_gate[0].reshape(32, 128).T
    sm[:, C_WCONV:C_WCONV + 192] = w_conv_qkv[0].reshape(4, 48, 128).transpose(2, 1, 0).reshape(128, 192)
    sm[:, C_WDW:C_WDW + 496] = w_dw[0].reshape(CK, 16, 128).transpose(2, 1, 0).reshape(128, 496)
    sm[:, C_BDW:C_BDW + 16] = b_dw[0].reshape(16, 128).T
    sm[:, C_LNG:C_LNG + 16] = ln_g[0].reshape(16, 128).T
    sm[:, C_LNB:C_LNB + 16] = ln_b[0].reshape(16, 128).T
    sm[:, C_GDN] = g_dn_out[0]
    sm[:, C_ALOG:C_ALOG + 16] = np.broadcast_to(a_log[0][None, :], (128, 16))
    sm[:, C_DTB:C_DTB + 16] = np.broadcast_to(dt_bias[0][None, :], (128, 16))
    rows = np.zeros((128, 2 * D), np.float32)
    rows[:, 0:D] = np.broadcast_to(g_post[0][None, :], (128, D))
    rows[:, D:] = np.broadcast_to(g_ple[0][None, :], (128, D))
    i = np.arange(128)
    cst = np.zeros((128, NCST, 128), np.float32)
    cst[:, 0] = np.eye(128)
    cst[:, 1] = (i[:, None] <= i[None, :])
    cst[:, 2] = (i[:, None] > i[None, :])
    cst[:, 3] = 1.0
    cst[:, 4] = NEG * (i[:, None] < i[None, :])
    cst[:, 5] = (i[:, None] > i[None, :])
    for sg in range(7):
        bsz = 1 << sg
        same = (i[:, None] // (2 * bsz)) == (i[None, :] // (2 * bsz))
        mE = same & ((i[:, None] % (2 * bsz)) >= bsz) & ((i[None, :] % (2 * bsz)) < bsz)
        cst[:, 6 + sg] = mE
        cst[:, 13 + sg] = mE.T
    return sm, rows, cst.reshape(128, NCST * 128)


_NC_CACHE = {}


def kernel(x, p, g_pre, w_in, b_gate, w_conv_qkv, a_log, dt_bias, g_dn_out, w_dw, b_dw,
           ln_g, ln_b, w_br_a, w_br_b, w_out, g_post, w_ple_gate, w_ple_proj, g_ple):
    x = np.asarray(x, np.float32)
    p = np.asarray(p, np.float32)
    B, S, _ = x.shape
    TB = 512 if S % 512 == 0 else 128
    f = lambda a: np.ascontiguousarray(np.asarray(a, np.float32))
    sm, rows, cst = host_consts(*[np.asarray(a, np.float32) for a in
                                  (g_pre, b_gate, w_conv_qkv, a_log, dt_bias, g_dn_out, w_dw, b_dw, ln_g, ln_b, g_post, g_ple)])
    key = (S, TB)
    if key not in _NC_CACHE:
        _NC_CACHE[key] = build_nc(S, TB)
    nc = _NC_CACHE[key]
    shared = {"w_in": f(w_in[0]), "w_br_a": f(w_br_a[0]), "w_br_b": f(w_br_b[0]), "w_out": f(w_out[0]),
              "w_ple_gate": f(w_ple_gate[0]), "w_ple_proj": f(w_ple_proj[0]), "smalls": sm, "rows": rows, "cst": cst}
    in_maps = []
    for b in range(B):
        m = dict(shared)
        m["x"] = f(x[b])
        m["p"] = f(p[0, b])
        in_maps.append(m)
    res = run_bass_kernel_spmd(nc, in_maps, core_ids=list(range(B)))
    return np.stack([np.asarray(r["out"], np.float32) for r in res.results], axis=0)
```

```python
import math
from contextlib import ExitStack

import numpy as np
import concourse.bass as bass
import concourse.mybir as mybir
from concourse.bass_utils import run_bass_kernel_spmd

F32 = mybir.dt.float32
BF = mybir.dt.bfloat16
AF = mybir.ActivationFunctionType
ALU = mybir.AluOpType
AX = mybir.AxisListType

D = 2048
KC = 16
H = 16
PLE = 256
INC = 18464
OFF_Q, OFF_K, OFF_V, OFF_ZA, OFF_BG, OFF_GLU, OFF_ZB, OFF_GATE = 0, 2048, 4096, 6144, 8192, 8224, 12320, 14368
EPS = 1e-6
CK = 31

C_GPRE = 0
C_BGATE = 16
C_WCONV = 48
C_WDW = 240
C_BDW = 736
C_LNG = 752
C_LNB = 768
C_GDN = 784
C_ALOG = 785
C_DTB = 801
NS = 820
NEG = -30000.0
NCST = 20
KSTOP = 9
MMG = 16
KPAIRS = 99
KSTEPS = 999
KOFF = -1
KWIN = 1


class Buf:
    __slots__ = ("w", "r", "name")

    def __init__(self, name=""):
        self.w = None
        self.r = {}
        self.name = name


class Rec:
    __slots__ = ("waits_c", "waits_d", "fn", "dma_sem", "needed", "val")

    def __init__(self, waits_c, waits_d, fn, dma_sem):
        self.waits_c = waits_c
        self.waits_d = waits_d
        self.fn = fn
        self.dma_sem = dma_sem
        self.needed = False
        self.val = 0


class EngState:
    def __init__(self, name):
        self.name = name
        self.ops = []
        self.seen_c = {}
        self.seen_d = {}
        self.last_c = -1


class Prog:
    ENGS = ("pe", "act", "dve", "pool", "sp")

    def __init__(self):
        self.E = {n: EngState(n) for n in self.ENGS}
        self.dcount = {}

    def op(self, eng, fn, reads=(), writes=(), dma_sem=None):
        e = self.E[eng]
        need_c = {}
        need_d = {}

        def add(tok, raw):
            if tok is None:
                return
            if tok[0] == "c":
                en, idx = tok[1], tok[2]
                if en == eng and dma_sem is None:
                    if eng == "pe":
                        return
                if e.seen_c.get(en, -1) >= idx:
                    return
                if need_c.get(en, -1) < idx:
                    need_c[en] = idx
            else:
                sk, val = tok[1], tok[2]
                if e.seen_d.get(sk, 0) >= val:
                    return
                if need_d.get(sk, 0) < val:
                    need_d[sk] = val

        for b in reads:
            add(b.w, True)
        for b in writes:
            add(b.w, False)
            for t in b.r.values():
                add(t, False)
        for en, idx in need_c.items():
            e.seen_c[en] = idx
            self.E[en].ops[idx].needed = True
        for sk, val in need_d.items():
            e.seen_d[sk] = val
        rec = Rec(need_c, need_d, fn, dma_sem)
        e.ops.append(rec)
        idx = len(e.ops) - 1
        if dma_sem is None:
            tok = ("c", eng, idx)
            key = eng
            e.last_c = idx
        else:
            self.dcount[dma_sem] = self.dcount.get(dma_sem, 0) + 16
            tok = ("d", dma_sem, self.dcount[dma_sem])
            key = ("d", dma_sem)
        for b in reads:
            b.r[key] = tok
        for b in writes:
            b.w = tok
            b.r = {}

    def barrier(self):
        for eng in self.ENGS:
            e = self.E[eng]
            need_c = {}
            need_d = {}
            for en in self.ENGS:
                o = self.E[en]
                if en == eng or o.last_c < 0:
                    continue
                if e.seen_c.get(en, -1) < o.last_c:
                    need_c[en] = o.last_c
                    e.seen_c[en] = o.last_c
                    o.ops[o.last_c].needed = True
            for sk, val in self.dcount.items():
                if e.seen_d.get(sk, 0) < val:
                    need_d[sk] = val
                    e.seen_d[sk] = val
            if need_c or need_d:
                e.ops.append(Rec(need_c, need_d, None, None))

    def finalize(self):
        for e in self.E.values():
            cum = 0
            for rec in e.ops:
                if rec.fn is not None and rec.dma_sem is None and rec.needed:
                    cum += 1
                    rec.val = cum

    def replay(self, eng, h, esem, dsem):
        e = self.E[eng]
        for rec in e.ops:
            for en, idx in rec.waits_c.items():
                h.wait_ge(esem[en], self.E[en].ops[idx].val)
            for sk, val in rec.waits_d.items():
                h.wait_ge(dsem[sk], val)
            if rec.fn is None:
                continue
            ins = rec.fn(h)
            if rec.dma_sem is not None:
                ins.then_inc(dsem[rec.dma_sem], 16)
            elif rec.needed:
                ins.then_inc(esem[eng], 1)


class T:
    __slots__ = ("ap", "b")

    def __init__(self, ap, name=""):
        self.ap = ap
        self.b = Buf(name)


def build_nc(S, TB, dbg=False):
    NB = TB // 128
    NBLK = S // TB
    assert TB % 128 == 0 and TB <= 512 and S % TB == 0
    nc = bass.Bass("TRN2", target_bir_lowering=False)
    x_d = nc.dram_tensor("x", [S, D], F32, kind="ExternalInput").ap()
    p_d = nc.dram_tensor("p", [S, PLE], F32, kind="ExternalInput").ap()
    w_in_d = nc.dram_tensor("w_in", [D, INC], F32, kind="ExternalInput").ap()
    w_bra_d = nc.dram_tensor("w_br_a", [D, D], F32, kind="ExternalInput").ap()
    w_brb_d = nc.dram_tensor("w_br_b", [D, D], F32, kind="ExternalInput").ap()
    w_out_d = nc.dram_tensor("w_out", [D, D], F32, kind="ExternalInput").ap()
    w_pg_d = nc.dram_tensor("w_ple_gate", [D, D], F32, kind="ExternalInput").ap()
    w_pp_d = nc.dram_tensor("w_ple_proj", [PLE, D], F32, kind="ExternalInput").ap()
    sm_d = nc.dram_tensor("smalls", [128, NS], F32, kind="ExternalInput").ap()
    rows_d = nc.dram_tensor("rows", [128, 2 * D], F32, kind="ExternalInput").ap()
    cst_d = nc.dram_tensor("cst", [128, NCST * 128], F32, kind="ExternalInput").ap()
    out_d = nc.dram_tensor("out", [S, D], F32, kind="ExternalOutput").ap()
    dbg_d = None
    if dbg:
        dbg_d = nc.dram_tensor("dbg", [128, 4, 16, TB], F32, kind="ExternalOutput").ap()

    P = Prog()
    NW = 8
    ARENA_F32 = 51200

    es = ExitStack()
    arena = es.enter_context(nc.sbuf_tensor("arena", [128, ARENA_F32], F32))
    psum = es.enter_context(nc.psum_tensor("psum", [128, 8, 512], F32))
    PS = [T(psum[:, i, :], f"ps{i}") for i in range(8)]

    class Alloc:
        def __init__(self, base, limit):
            self.base = base
            self.off = base
            self.limit = limit

        def reset(self):
            self.off = self.base

        def get(self, shape, dt, name=""):
            n = 1
            for s in shape[1:]:
                n *= s
            nbytes = n * (4 if dt == F32 else 2)
            nw = (nbytes + 3) // 4
            assert self.off + nw <= self.limit, (name, self.off, nw, self.limit)
            ap = arena[:, self.off:self.off + nw]
            self.off += nw
            if dt != F32:
                ap = ap.bitcast(dt)
                if nbytes % 4:
                    ap = ap[:, 0:n]
            if len(shape) == 3:
                ap = ap.rearrange("p (a b) -> p a b", a=shape[1])
            elif len(shape) == 4:
                ap = ap.rearrange("p (a b c) -> p a b c", a=shape[1], b=shape[2])
            return T(ap, name)

    pa = Alloc(0, ARENA_F32)
    cb = pa.get([128, NCST, 128], BF, "cb")
    identb, uleb, ugtb, onesb = cb.ap[:, 0, :], cb.ap[:, 1, :], cb.ap[:, 2, :], cb.ap[:, 3, :]
    negm4 = pa.get([128, NB, 128], BF, "negm4")
    strict4 = pa.get([128, NB, 128], F32, "strict4")
    sm = pa.get([128, NS], F32, "sm")
    negA = pa.get([128, 16], F32, "negA")
    tails_qkv = pa.get([128, 48, 3], F32, "tails_qkv")
    tails_u = pa.get([128, 16, 30], BF, "tails_u")
    Sst = pa.get([128, 16, 128], F32, "Sst")
    Sst_h = [Buf(f"S{h}") for h in range(H)]
    wpool = pa.get([128, NW, 16, 128], BF, "wpool")
    wslot = [Buf(f"w{i}") for i in range(NW)]
    arX0 = pa.off
    hT = pa.get([128, KC, TB], BF, "hT")
    oaT = pa.get([128, KC, TB], BF, "oaT")
    arX1 = pa.off
    arY0 = pa.off
    ubT = pa.get([128, KC, TB], BF, "ubT")
    arZ0 = pa.off
    minT = pa.get([128, KC, TB], BF, "minT")
    alX = Alloc(arX0, arX1)
    mixed = alX.get([128, NB, D], F32, "mixed")
    alY = Alloc(arY0, arZ0)
    x1T = alY.get([128, KC, TB], BF, "x1T")
    wk = Alloc(pa.off, ARENA_F32)

    wptr = [0]
    bigp = [0]
    smallp = [0]

    def big():
        b = PS[bigp[0] % 3]
        bigp[0] += 1
        return b

    def small():
        b = PS[3 + smallp[0] % 5]
        smallp[0] += 1
        return b

    def mm(out, lhsT, rhs, start, stop, R, W):
        P.op("pe", lambda t: t.matmul(out, lhsT=lhsT, rhs=rhs, start=start, stop=stop), R, W)

    def act(out, in_, func, R, W, **kw):
        P.op("act", lambda s: s.activation(out=out, in_=in_, func=func, **kw), R, W)

    def tt(out, in0, in1, op, R, W, eng="dve"):
        P.op(eng, lambda v: v.tensor_tensor(out=out, in0=in0, in1=in1, op=op), R, W)

    def ts(out, in0, s1, op0, R, W, s2=None, op1=None, eng="dve"):
        if op1 is None:
            P.op(eng, lambda v: v.tensor_scalar(out=out, in0=in0, scalar1=s1, scalar2=None, op0=op0), R, W)
        else:
            P.op(eng, lambda v: v.tensor_scalar(out=out, in0=in0, scalar1=s1, scalar2=s2, op0=op0, op1=op1), R, W)

    def stt(out, in0, scalar, in1, op0, op1, R, W):
        P.op("dve", lambda v: v.scalar_tensor_tensor(out=out, in0=in0, scalar=scalar, in1=in1, op0=op0, op1=op1), R, W)

    def cp(out, in_, R, W, eng="dve"):
        if eng == "act":
            P.op("act", lambda s: s.activation(out=out, in_=in_, func=AF.Copy), R, W)
        else:
            P.op(eng, lambda v: v.tensor_copy(out=out, in_=in_), R, W)

    def memset(ap, val, W):
        P.op("dve", lambda v: v.memset(ap, val), (), W)

    def dma(eng, out, in_, R, W, sem):
        P.op(eng, lambda q: q.dma_start(out=out, in_=in_), R, W, dma_sem=sem)

    def load_w(src, ncols=128, k=D):
        s = wptr[0] % NW
        wptr[0] += 1
        nk = k // 128
        dma("pool", wpool.ap[:, s, 0:nk, 0:ncols], src.rearrange("(kc p) c -> p kc c", p=128), (), (wslot[s],), f"w{s}")
        return s

    def load_wide(src512):
        while wptr[0] % 4:
            wptr[0] += 1
        s0 = wptr[0] % NW
        for j in range(4):
            load_w(src512[:, j * 128:(j + 1) * 128])
        return s0

    def rsqrt_small(out, in_, R, W, tmp, scale=1.0, post_bias=0.0):
        act(tmp.ap, in_, AF.Ln, R, (tmp.b,), scale=scale, bias=EPS)
        act(out, tmp.ap, AF.Exp, (tmp.b,), W, scale=-0.5, bias=post_bias)

    def bc(ap2, n):
        return ap2.unsqueeze(2).to_broadcast([128, ap2.shape[1], n])

    def bcm(ap2, a):
        return ap2.unsqueeze(1).to_broadcast([128, a, ap2.shape[1]])

    wk.reset()
    cstf = wk.get([128, NCST, 128], F32, "cstf")
    dma("sp", cstf.ap, cst_d.rearrange("p (a b) -> p a b", a=NCST), (), (cstf.b,), "cst")
    dma("sp", sm.ap, sm_d, (), (sm.b,), "sm")
    cp(cb.ap, cstf.ap, (cstf.b,), (cb.b,))
    cp(negm4.ap, bcm(cstf.ap[:, 4, :], NB), (cstf.b,), (negm4.b,))
    cp(strict4.ap, bcm(cstf.ap[:, 5, :], NB), (cstf.b,), (strict4.b,))
    act(negA.ap, sm.ap[:, C_ALOG:C_ALOG + 16], AF.Exp, (sm.b,), (negA.b,))
    ts(negA.ap, negA.ap, -1.0, ALU.mult, (negA.b,), (negA.b,))
    memset(tails_qkv.ap, 0.0, (tails_qkv.b,))
    memset(tails_u.ap, 0.0, (tails_u.b,))
    memset(Sst.ap, 0.0, [Sst.b] + Sst_h)
    P.barrier()

    def smc(c):
        return sm.ap[:, c:c + 1]

    for blk in range(NBLK):
        t0 = blk * TB
        wk.reset()
        xt = [wk.get([128, D], F32, f"xt{i}") for i in range(2)]
        xjunk = wk.get([128, D], BF, "xjunk")
        xn = wk.get([128, D], BF, "xn")
        ss0 = wk.get([128, NB], F32, "ss0")
        l0 = wk.get([128, NB], F32, "l0")
        rstd0 = wk.get([128, NB], F32, "rstd0")
        hTk = [Buf(f"hT{n}") for n in range(NB)]
        for n in range(NB):
            xs = xt[n % 2]
            dma("sp", xs.ap, x_d[t0 + n * 128:t0 + (n + 1) * 128, :], (), (xs.b,), f"xt{n % 2}")
            act(xjunk.ap, xs.ap, AF.Square, (xs.b,), (xjunk.b, ss0.b), accum_out=ss0.ap[:, n:n + 1])
            act(l0.ap[:, n:n + 1], ss0.ap[:, n:n + 1], AF.Ln, (ss0.b,), (l0.b,), scale=1.0 / D, bias=EPS)
            act(rstd0.ap[:, n:n + 1], l0.ap[:, n:n + 1], AF.Exp, (l0.b,), (rstd0.b,), scale=-0.5)
            ts(xn.ap, xs.ap, rstd0.ap[:, n:n + 1], ALU.mult, (xs.b, rstd0.b), (xn.b,))
            for g in range(4):
                bk = big()
                for j in range(4):
                    kc = g * 4 + j
                    mm(bk.ap[:, j * 128:(j + 1) * 128], xn.ap[:, kc * 128:(kc + 1) * 128], identb, True, True,
                       (xn.b, cb.b), (bk.b,))
                tt(hT.ap[:, g * 4:g * 4 + 4, n * 128:(n + 1) * 128], bk.ap.rearrange("p (a b) -> p a b", a=4),
                   bc(sm.ap[:, C_GPRE + g * 4:C_GPRE + g * 4 + 4], 128), ALU.mult, (bk.b, sm.b), (hTk[n],))
        HT = tuple(hTk)

        if KSTOP < 1:
            continue
        P.barrier()
        wk.reset()
        bgs = wk.get([128, NB, 32], F32, "bgs")
        beta = wk.get([128, NB, 16], F32, "beta")
        t16 = wk.get([128, NB, 16], F32, "t16")
        gg = wk.get([128, NB, 16], F32, "gg")
        ghl = wk.get([128, NB, 32], BF, "ghl")
        egkl = wk.get([128, NB, 48], F32, "egkl")
        sw = load_w(w_in_d[:, OFF_BG:OFF_BG + 32], ncols=32)
        for n in range(NB):
            bk = small()
            for kc in range(KC):
                mm(bk.ap[:, 0:32], hT.ap[:, kc, n * 128:(n + 1) * 128], wpool.ap[:, sw, kc, 0:32], kc == 0, kc == KC - 1,
                   (hTk[n], wslot[sw]), (bk.b,))
            cp(bgs.ap[:, n, :], bk.ap[:, 0:32], (bk.b,), (bgs.b,), eng="act")
        act(beta.ap, bgs.ap[:, :, 0:16], AF.Sigmoid, (bgs.b,), (beta.b,))
        tt(t16.ap, bgs.ap[:, :, 16:32], bcm(sm.ap[:, C_DTB:C_DTB + 16], NB), ALU.add, (bgs.b, sm.b), (t16.b,))
        act(t16.ap, t16.ap, AF.Exp, (t16.b,), (t16.b,))
        act(t16.ap, t16.ap, AF.Ln, (t16.b,), (t16.b,), bias=1.0)
        tt(gg.ap, t16.ap, bcm(negA.ap, NB), ALU.mult, (t16.b, negA.b), (gg.b,))
        cp(ghl.ap[:, :, 0:16], gg.ap, (gg.b,), (ghl.b,))
        tt(ghl.ap[:, :, 16:32], gg.ap, ghl.ap[:, :, 0:16], ALU.subtract, (gg.b, ghl.b), (ghl.b,))
        for n in range(NB):
            bk = small()
            for q, lt in enumerate((uleb, ugtb, onesb)):
                mm(bk.ap[:, q * 16:(q + 1) * 16], lt, ghl.ap[:, n, 0:16], True, False, (cb.b, ghl.b), (bk.b,))
                mm(bk.ap[:, q * 16:(q + 1) * 16], lt, ghl.ap[:, n, 16:32], False, True, (cb.b, ghl.b), (bk.b,))
            act(egkl.ap[:, n, :], bk.ap[:, 0:48], AF.Exp, (bk.b,), (egkl.b,))

        pre = [wk.get([128, TB + 3], F32, f"pre{i}") for i in range(2)]
        cacc = [wk.get([128, TB], F32, f"cacc{i}") for i in range(2)]
        s1 = [[wk.get([128, TB], BF, f"s1_{par}_{i}") for i in range(4)] for par in range(4)]
        sqs = wk.get([128, NB, 128], F32, "sqs")
        ojunk = wk.get([128, 128], BF, "ojunk")

        def nsl(n):
            return slice(n * 128, (n + 1) * 128)

        def stage1(h):
            par = h % 4
            steps = []

            def mk(idx, off):
                st = {}
                tix = off // 128 + h

                def post():
                    bk = st["bk"]
                    if idx < 3:
                        pr = pre[idx % 2]
                        cp(pr.ap[:, 0:3], tails_qkv.ap[:, tix, :], (tails_qkv.b,), (pr.b,))
                        cp(pr.ap[:, 3:3 + TB], bk.ap[:, 0:TB], (bk.b,), (pr.b,), eng="act")
                        cp(tails_qkv.ap[:, tix, :], pr.ap[:, TB:TB + 3], (pr.b,), (tails_qkv.b,))
                        ac = cacc[idx % 2]
                        wc = C_WCONV + tix * 4
                        ts(ac.ap, pr.ap[:, 0:TB], smc(wc), ALU.mult, (pr.b, sm.b), (ac.b,))
                        for k in range(1, 4):
                            stt(ac.ap, pr.ap[:, k:k + TB], smc(wc + k), ac.ap, ALU.mult, ALU.add, (pr.b, sm.b, ac.b), (ac.b,))
                        act(s1[par][idx].ap, ac.ap, AF.Silu, (ac.b,), (s1[par][idx].b,))
                    else:
                        act(s1[par][3].ap, bk.ap[:, 0:TB], AF.Silu, (bk.b,), (s1[par][3].b,))

                def grp(k0, k1):
                    def f():
                        if k0 == 0:
                            st["s"] = pre_slots[h][idx]
                            st["bk"] = big()
                        s_, bk = st["s"], st["bk"]
                        for kc in range(k0, k1):
                            mm(bk.ap[:, 0:TB], wpool.ap[:, s_, kc, :], hT.ap[:, kc, :], kc == 0, kc == KC - 1,
                               HT + (wslot[s_],), (bk.b,))
                        if k1 == KC:
                            post()
                            s1_done[h] = s1_done.get(h, 0) + 1
                    return f

                return [grp(k, k + MMG) for k in range(0, KC, MMG)]

            for idx, off in enumerate((OFF_Q, OFF_K, OFF_V, OFF_ZA)):
                steps += mk(idx, off)
            return steps

        def make_env(tag):
            ks_tok = wk.get([128, NB, 128], BF, "ks_tok")
            qs_tok = wk.get([128, NB, 128], BF, "qs_tok")
            vb = wk.get([128, NB, 128], BF, "vb")
            ssk = wk.get([128, NB], F32, "ssk")
            ssq = wk.get([128, NB], F32, "ssq")
            ltmp = wk.get([128, NB], F32, "ltmp")
            rs_k = wk.get([128, NB], F32, "rs_k")
            rs_q = wk.get([128, NB], F32, "rs_q")
            c_t = wk.get([128, NB], F32, "c_t")
            c_kbg = wk.get([128, NB], F32, "c_kbg")
            c_ke = wk.get([128, NB], F32, "c_ke")
            c_qd = wk.get([128, NB], F32, "c_qd")
            Dk = wk.get([128, NB, 128], BF, "Dk")
            Dq = wk.get([128, NB, 128], BF, "Dq")
            Dqd = wk.get([128, NB, 128], BF, "Dqd")
            Do = Dk
            knT = wk.get([128, TB], BF, "knT")
            qnT = wk.get([128, TB], BF, "qnT")
            qdT = wk.get([128, TB], BF, "qdT")
            kbg = wk.get([128, NB, 128], BF, "kbg")
            ke = wk.get([128, NB, 128], BF, "ke")
            Rm = wk.get([128, NB, 128], BF, "Rm")
            SBm = wk.get([128, NB, 128], BF, "SBm")
            Mfull = wk.get([128, NB, 128], BF, "Mfull")
            Mb = wk.get([128, NB, 128], BF, "Mb")
            Am = wk.get([128, NB, 128], BF, "Am")
            Bm = wk.get([128, NB, 128], BF, "Bm")
            Tm = wk.get([128, NB, 128], BF, "Tm")
            Xm = wk.get([128, NB, 128], BF, "Xm")
            Yn = wk.get([128, NB, 128], BF, "Yn")
            Zn = wk.get([128, NB, 128], BF, "Zn")
            Eb = [wk.get([128, NB, 128], BF, f"Eb{i}") for i in range(2)]
            Fb = [wk.get([128, NB, 128], BF, f"Fb{i}") for i in range(2)]
            QKm = Rm
            QKmT = wk.get([128, NB, 128], BF, "QKmT")
            nwT = qs_tok
            o_tok = ks_tok
            vn = [wk.get([128, 128], BF, f"vn{i}") for i in range(2)]
            S_bf = wk.get([128, 128], BF, "S_bf")
            sso = wk.get([128, NB], F32, "sso")
            rso = wk.get([128, NB], F32, "rso")

            def stage2(h):
                par = h % 4
                qsT, ksT, vsT, zsT = s1[par]
                steps = []
                beta_h = beta.ap[:, :, h]
                eg_h = egkl.ap[:, :, h]
                ek_h = egkl.ap[:, :, 16 + h]

                def sa():
                    for src, dst, ssx in ((ksT, ks_tok, ssk), (qsT, qs_tok, ssq)):
                        bk = small()
                        for n in range(NB):
                            mm(bk.ap[:, nsl(n)], src.ap[:, nsl(n)], identb, True, True, (src.b, cb.b), (bk.b,))
                        cp(dst.ap, bk.ap[:, 0:TB].rearrange("p (a b) -> p a b", a=NB), (bk.b,), (dst.b,), eng="act")
                        tt(sqs.ap, dst.ap, dst.ap, ALU.mult, (dst.b,), (sqs.b,))
                        P.op("dve", lambda v, o=ssx.ap, i=sqs.ap: v.tensor_reduce(out=o, in_=i, op=ALU.add, axis=AX.X),
                             (sqs.b,), (ssx.b,))
                    bk = small()
                    for n in range(NB):
                        mm(bk.ap[:, nsl(n)], vsT.ap[:, nsl(n)], identb, True, True, (vsT.b, cb.b), (bk.b,))
                    tt(vb.ap, bk.ap[:, 0:TB].rearrange("p (a b) -> p a b", a=NB), bc(beta_h, 128), ALU.mult,
                       (bk.b, beta.b), (vb.b,))

                def sb():
                    rsqrt_small(rs_k.ap, ssk.ap, (ssk.b,), (rs_k.b,), ltmp)
                    rsqrt_small(rs_q.ap, ssq.ap, (ssq.b,), (rs_q.b,), ltmp, post_bias=-0.5 * math.log(128.0))
                    tt(c_t.ap, rs_k.ap, beta_h, ALU.mult, (rs_k.b, beta.b), (c_t.b,))
                    tt(c_kbg.ap, c_t.ap, eg_h, ALU.mult, (c_t.b, egkl.b), (c_kbg.b,))
                    tt(c_ke.ap, rs_k.ap, ek_h, ALU.mult, (rs_k.b, egkl.b), (c_ke.b,))
                    tt(c_qd.ap, rs_q.ap, eg_h, ALU.mult, (rs_q.b, egkl.b), (c_qd.b,))
                    idb = bcm(identb, NB)
                    tt(Dk.ap, idb, bc(rs_k.ap, 128), ALU.mult, (cb.b, rs_k.b), (Dk.b,))
                    tt(Dq.ap, idb, bc(rs_q.ap, 128), ALU.mult, (cb.b, rs_q.b), (Dq.b,))
                    tt(Dqd.ap, idb, bc(c_qd.ap, 128), ALU.mult, (cb.b, c_qd.b), (Dqd.b,))
                    tt(kbg.ap, ks_tok.ap, bc(c_kbg.ap, 128), ALU.mult, (ks_tok.b, c_kbg.b), (kbg.b,))
                    tt(ke.ap, ks_tok.ap, bc(c_ke.ap, 128), ALU.mult, (ks_tok.b, c_ke.b), (ke.b,))

                    tt(Rm.ap, bcm(ugtb, NB), bc(ghl.ap[:, :, h], 128), ALU.mult, (cb.b, ghl.b), (Rm.b,))
                    tt(SBm.ap, strict4.ap, bc(beta_h, 128), ALU.mult, (strict4.b, beta.b), (SBm.b,))

                def sc():
                    for src, dg, dst in ((ks_tok, Dk, knT), (qs_tok, Dq, qnT), (qs_tok, Dqd, qdT)):
                        bk = small()
                        for n in range(NB):
                            mm(bk.ap[:, nsl(n)], src.ap[:, n, :], dg.ap[:, n, :], True, True, (src.b, dg.b), (bk.b,))
                        cp(dst.ap, bk.ap[:, 0:TB], (bk.b,), (dst.b,), eng="act")
                    bk = small()
                    mm(bk.ap[:, 0:TB], uleb, Rm.ap.rearrange("p a b -> p (a b)"), True, False, (cb.b, Rm.b), (bk.b,))
                    mm(bk.ap[:, 0:TB], identb, negm4.ap.rearrange("p a b -> p (a b)"), False, True, (cb.b, negm4.b), (bk.b,))
                    act(Mfull.ap, bk.ap[:, 0:TB].rearrange("p (a b) -> p a b", a=NB), AF.Exp, (bk.b,), (Mfull.b,))
                    tt(Mb.ap, Mfull.ap, SBm.ap, ALU.mult, (Mfull.b, SBm.b), (Mb.b,))

                def sd():
                    bk = small()
                    for n in range(NB):
                        mm(bk.ap[:, nsl(n)], knT.ap[:, nsl(n)], knT.ap[:, nsl(n)], True, True, (knT.b,), (bk.b,))
                    tt(Am.ap, bk.ap[:, 0:TB].rearrange("p (a b) -> p a b", a=NB), Mb.ap, ALU.mult, (bk.b, Mb.b), (Am.b,))
                    bk = small()
                    for n in range(NB):
                        mm(bk.ap[:, nsl(n)], qnT.ap[:, nsl(n)], knT.ap[:, nsl(n)], True, True, (qnT.b, knT.b), (bk.b,))
                    tt(QKm.ap, bk.ap[:, 0:TB].rearrange("p (a b) -> p a b", a=NB), Mfull.ap, ALU.mult, (bk.b, Mfull.b), (QKm.b,))

                def se():
                    bk = small()
                    for n in range(NB):
                        mm(bk.ap[:, nsl(n)], Am.ap[:, n, :], identb, True, True, (Am.b, cb.b), (bk.b,))
                    cp(Bm.ap, bk.ap[:, 0:TB].rearrange("p (a b) -> p a b", a=NB), (bk.b,), (Bm.b,), eng="act")
                    bk = small()
                    for n in range(NB):
                        mm(bk.ap[:, nsl(n)], QKm.ap[:, n, :], identb, True, True, (QKm.b, cb.b), (bk.b,))
                    cp(QKmT.ap, bk.ap[:, 0:TB].rearrange("p (a b) -> p a b", a=NB), (bk.b,), (QKmT.b,), eng="act")

                def st0():
                    tt(Eb[0].ap, Am.ap, bcm(cb.ap[:, 6, :], NB), ALU.mult, (Am.b, cb.b), (Eb[0].b,), eng="pool")
                    tt(Fb[0].ap, Bm.ap, bcm(cb.ap[:, 13, :], NB), ALU.mult, (Bm.b, cb.b), (Fb[0].b,), eng="pool")
                    tt(Tm.ap, bcm(identb, NB), Eb[0].ap, ALU.subtract, (cb.b, Eb[0].b), (Tm.b,))
                    tt(Xm.ap, bcm(identb, NB), Fb[0].ap, ALU.subtract, (cb.b, Fb[0].b), (Xm.b,))

                def mkstage(sg):
                    def f():
                        E = Eb[sg % len(Eb)]
                        F = Fb[sg % len(Fb)]
                        last = sg == 6
                        v3 = lambda bk: bk.ap[:, 0:TB].rearrange("p (a b) -> p a b", a=NB)
                        tt(E.ap, Am.ap, bcm(cb.ap[:, 6 + sg, :], NB), ALU.mult, (Am.b, cb.b), (E.b,), eng="pool")
                        if not last:
                            tt(F.ap, Bm.ap, bcm(cb.ap[:, 13 + sg, :], NB), ALU.mult, (Bm.b, cb.b), (F.b,), eng="pool")
                        bY = small()
                        for n in range(NB):
                            mm(bY.ap[:, nsl(n)], E.ap[:, n, :], Xm.ap[:, n, :], True, True, (E.b, Xm.b), (bY.b,))
                        act(Yn.ap, v3(bY), AF.Identity, (bY.b,), (Yn.b,), scale=-1.0)
                        if not last:
                            bZ = small()
                            for n in range(NB):
                                mm(bZ.ap[:, nsl(n)], F.ap[:, n, :], Tm.ap[:, n, :], True, True, (F.b, Tm.b), (bZ.b,))
                            act(Zn.ap, v3(bZ), AF.Identity, (bZ.b,), (Zn.b,), scale=-1.0)
                        bX = small()
                        for n in range(NB):
                            mm(bX.ap[:, nsl(n)], identb, Xm.ap[:, n, :], True, False, (cb.b, Xm.b), (bX.b,))
                            mm(bX.ap[:, nsl(n)], Tm.ap[:, n, :], Yn.ap[:, n, :], False, True, (Tm.b, Yn.b), (bX.b,))
                        if not last:
                            bT = small()
                            for n in range(NB):
                                mm(bT.ap[:, nsl(n)], identb, Tm.ap[:, n, :], True, False, (cb.b, Tm.b), (bT.b,))
                                mm(bT.ap[:, nsl(n)], Xm.ap[:, n, :], Zn.ap[:, n, :], False, True, (Xm.b, Zn.b), (bT.b,))
                        cp(Xm.ap, v3(bX), (bX.b,), (Xm.b,))
                        if not last:
                            cp(Tm.ap, v3(bT), (bT.b,), (Tm.b,))
                    return f

                def sw_():
                    bk = small()
                    for n in range(NB):
                        mm(bk.ap[:, nsl(n)], kbg.ap[:, n, :], Xm.ap[:, n, :], True, True, (kbg.b, Xm.b), (bk.b,))
                    ts(nwT.ap, bk.ap[:, 0:TB].rearrange("p (a b) -> p a b", a=NB), -1.0, ALU.mult, (bk.b,), (nwT.b,))
                    cp(S_bf.ap, Sst.ap[:, h, :], (Sst_h[h],), (S_bf.b,), eng="act")

                def mkrec(n):
                    def f():
                        bk = small()
                        v = vn[n % 2]
                        mm(bk.ap[:, 0:128], Xm.ap[:, n, :], vb.ap[:, n, :], True, False, (Xm.b, vb.b), (bk.b,))
                        mm(bk.ap[:, 0:128], nwT.ap[:, n, :], S_bf.ap, False, True, (nwT.b, S_bf.b), (bk.b,))
                        cp(v.ap, bk.ap[:, 0:128], (bk.b,), (v.b,), eng="act")
                        mm(bk.ap[:, 128:256], ke.ap[:, n, :], v.ap, True, True, (ke.b, v.b), (bk.b,))
                        mm(bk.ap[:, 256:384], qdT.ap[:, nsl(n)], S_bf.ap, True, False, (qdT.b, S_bf.b), (bk.b,))
                        mm(bk.ap[:, 256:384], QKmT.ap[:, n, :], v.ap, False, True, (QKmT.b, v.b), (bk.b,))
                        stt(Sst.ap[:, h, :], Sst.ap[:, h, :], egkl.ap[:, n, 32 + h:33 + h], bk.ap[:, 128:256], ALU.mult, ALU.add,
                            (Sst_h[h], egkl.b, bk.b), (Sst_h[h],))
                        cp(S_bf.ap, Sst.ap[:, h, :], (Sst_h[h],), (S_bf.b,), eng="act")
                        cp(o_tok.ap[:, n, :], bk.ap[:, 256:384], (bk.b,), (o_tok.b,))
                        act(ojunk.ap, bk.ap[:, 256:384], AF.Square, (bk.b,), (ojunk.b, sso.b), accum_out=sso.ap[:, n:n + 1])
                    return f

                def sp_():
                    rsqrt_small(rso.ap, sso.ap, (sso.b,), (rso.b,), ltmp, scale=1.0 / 128)
                    tt(Do.ap, bcm(identb, NB), bc(rso.ap, 128), ALU.mult, (cb.b, rso.b), (Do.b,))
                    bk = small()
                    for n in range(NB):
                        mm(bk.ap[:, nsl(n)], o_tok.ap[:, n, :], Do.ap[:, n, :], True, True, (o_tok.b, Do.b), (bk.b,))
                    stt(oaT.ap[:, h, :], bk.ap[:, 0:TB], smc(C_GDN), zsT.ap, ALU.mult, ALU.mult, (bk.b, sm.b, zsT.b), (oaT.b,))

                steps += [sa, sb, sc, sd, se, st0]
                steps += [mkstage(sg) for sg in range(1, 7)]
                steps += [sw_]
                steps += [mkrec(n) for n in range(NB)]
                steps += [sp_]
                return steps

            return stage2

        envs = [make_env(0), make_env(1)]

        def interleave(a, b):
            out = []
            la, lb = len(a), len(b)
            if la == 0:
                return list(b)
            pos = [int((i + 0.5) * lb / la) for i in range(la)]
            ai = 0
            for j in range(lb + 1):
                while ai < la and pos[ai] == j:
                    out.append(a[ai])
                    ai += 1
                if j < lb:
                    out.append(b[j])
            return out

        pre_slots = {}

        s1_done = {}

        def prefetch(hh):
            if hh < H:
                assert hh < 2 or s1_done.get(hh - 2, 0) == 4, ("prefetch before stage1 done", hh)
                pre_slots[hh] = [load_w(w_in_d[:, off + hh * 128:off + (hh + 1) * 128])
                                 for off in (OFF_Q, OFF_K, OFF_V, OFF_ZA)]

        def zipsteps(a, b):
            out = []
            for i in range(max(len(a), len(b))):
                if i < len(a):
                    out.append(a[i])
                if i < len(b):
                    out.append(b[i])
            return out

        prefetch(0)
        prefetch(1)
        for f in stage1(0) + stage1(1):
            f()
        seqs = [[], []]
        for hh in range(H):
            st = envs[hh % 2](hh)
            seqs[hh % 2] += [(hh, i, f) for i, f in enumerate(st)]
        nst = len(seqs[0]) // (H // 2)
        off = nst // 2 if KOFF < 0 else KOFF
        merged = []
        for i in range(len(seqs[0]) + off):
            if i < len(seqs[0]):
                merged.append(seqs[0][i])
            j = i - off
            if 0 <= j < len(seqs[1]):
                merged.append(seqs[1][j])
        startpos = {}
        for pos, (hh, i, f) in enumerate(merged):
            if i == 0:
                startpos[hh] = pos
        inserts = {}
        for hh in range(2, H):
            fl = stage1(hh)
            w1 = startpos[hh] - 4
            w0 = max(0, w1 - KWIN)
            for q_, f in enumerate(fl):
                pos = w0 + int(q_ * (w1 - w0) / len(fl))
                inserts.setdefault(pos, []).append(f)
            ppos = max(0, w0 - nst // 2)
            inserts.setdefault(ppos, []).insert(0, (lambda hx=hh: prefetch(hx)))
        done_pf = set()
        for pos, (hh, i, f) in enumerate(merged):
            for g in inserts.get(pos, []):
                g()
            f()

        if dbg and blk == NBLK - 1:
            P.barrier()
            dt_ = wk.get([128, 16, TB], F32, "dbgt")
            cp(dt_.ap, oaT.ap, (oaT.b,), (dt_.b,))
            dma("sp", dbg_d[:, 0, :, :], dt_.ap, (dt_.b,), (), "dbg")

        if KSTOP < 2:
            continue
        P.barrier()
        wk.reset()
        DW = [wk.get([128, CK, 128], BF, f"DW{i}") for i in range(2)]
        upre = [wk.get([128, TB + 30], BF, f"upre{i}") for i in range(2)]
        sgB = wk.get([128, TB], F32, "sgB")
        usq = [wk.get([128, TB], BF, f"usq{i}") for i in range(2)]
        acc1 = wk.get([128, TB], F32, "acc1")
        acc2 = wk.get([128, TB], F32, "acc2")
        meanB = wk.get([128, TB], F32, "meanB")
        varB = wk.get([128, TB], F32, "varB")
        rstdB = wk.get([128, TB], F32, "rstdB")
        nmrB = wk.get([128, TB], F32, "nmrB")
        szb = wk.get([128, TB], BF, "szb")
        t1 = wk.get([128, TB], F32, "t1")
        t3 = wk.get([128, TB], BF, "t3")
        ubk = [Buf(f"ub{c}") for c in range(KC)]
        memset(acc1.ap, 0.0, (acc1.b,))
        memset(acc2.ap, 0.0, (acc2.b,))
        def b_inproj(ct):
            up, dw = upre[ct % 2], DW[ct % 2]
            sa_ = load_w(w_in_d[:, OFF_GLU + ct * 128:OFF_GLU + (ct + 1) * 128])
            sb_ = load_w(w_in_d[:, OFF_GLU + D + ct * 128:OFF_GLU + D + (ct + 1) * 128])
            tt(dw.ap, bcm(identb, CK), bc(sm.ap[:, C_WDW + ct * CK:C_WDW + (ct + 1) * CK], 128), ALU.mult,
               (cb.b, sm.b), (dw.b,), eng="pool")
            bA = big()
            for kc in range(KC):
                mm(bA.ap[:, 0:TB], wpool.ap[:, sa_, kc, :], hT.ap[:, kc, :], kc == 0, kc == KC - 1, HT + (wslot[sa_],), (bA.b,))
            bB = big()
            for kc in range(KC):
                mm(bB.ap[:, 0:TB], wpool.ap[:, sb_, kc, :], hT.ap[:, kc, :], kc == 0, kc == KC - 1, HT + (wslot[sb_],), (bB.b,))
            act(sgB.ap, bB.ap[:, 0:TB], AF.Sigmoid, (bB.b,), (sgB.b,))
            cp(up.ap[:, 0:30], tails_u.ap[:, ct, :], (tails_u.b,), (up.b,))
            tt(up.ap[:, 30:30 + TB], bA.ap[:, 0:TB], sgB.ap, ALU.mult, (bA.b, sgB.b), (up.b,))
            cp(tails_u.ap[:, ct, :], up.ap[:, TB:TB + 30], (up.b,), (tails_u.b,))

        def b_conv(ct):
            up, dw, uq = upre[ct % 2], DW[ct % 2], usq[ct % 2]
            bC = big()
            for k in range(CK):
                mm(bC.ap[:, 0:TB], dw.ap[:, k, :], up.ap[:, k:k + TB], k == 0, k == CK - 1, (dw.b, up.b), (bC.b,))
            act(ubT.ap[:, ct, :], bC.ap[:, 0:TB], AF.Identity, (bC.b, sm.b), (ubk[ct],), bias=smc(C_BDW + ct))
            act(uq.ap, bC.ap[:, 0:TB], AF.Square, (bC.b, sm.b), (uq.b,), bias=smc(C_BDW + ct))

        def b_stats(ct):
            uq = usq[ct % 2]
            bS = small()
            mm(bS.ap[:, 0:TB], onesb, ubT.ap[:, ct, :], True, True, (cb.b, ubk[ct]), (bS.b,))
            tt(acc1.ap, acc1.ap, bS.ap[:, 0:TB], ALU.add, (acc1.b, bS.b), (acc1.b,))
            bS2 = small()
            mm(bS2.ap[:, 0:TB], onesb, uq.ap, True, True, (cb.b, uq.b), (bS2.b,))
            tt(acc2.ap, acc2.ap, bS2.ap[:, 0:TB], ALU.add, (acc2.b, bS2.b), (acc2.b,))

        b_inproj(0)
        for ct in range(KC):
            if ct + 1 < KC:
                b_inproj(ct + 1)
            b_conv(ct)
            if ct > 0:
                b_stats(ct - 1)
        b_stats(KC - 1)
        ts(meanB.ap, acc1.ap, 1.0 / D, ALU.mult, (acc1.b,), (meanB.b,))
        tt(varB.ap, meanB.ap, meanB.ap, ALU.mult, (meanB.b,), (varB.b,))
        stt(varB.ap, acc2.ap, 1.0 / D, varB.ap, ALU.mult, ALU.subtract, (acc2.b, varB.b), (varB.b,))
        act(t1.ap, varB.ap, AF.Ln, (varB.b,), (t1.b,), bias=EPS)
        act(rstdB.ap, t1.ap, AF.Exp, (t1.b,), (rstdB.b,), scale=-0.5)
        stt(nmrB.ap, meanB.ap, -1.0, rstdB.ap, ALU.mult, ALU.mult, (meanB.b, rstdB.b), (nmrB.b,))
        for ct in range(KC):
            sz = load_w(w_in_d[:, OFF_ZB + ct * 128:OFF_ZB + (ct + 1) * 128])
            bZ = big()
            for kc in range(KC):
                mm(bZ.ap[:, 0:TB], wpool.ap[:, sz, kc, :], hT.ap[:, kc, :], kc == 0, kc == KC - 1, HT + (wslot[sz],), (bZ.b,))
            act(szb.ap, bZ.ap[:, 0:TB], AF.Silu, (bZ.b,), (szb.b,))
            tt(t1.ap, ubT.ap[:, ct, :], rstdB.ap, ALU.mult, (ubk[ct], rstdB.b), (t1.b,))
            tt(t1.ap, t1.ap, nmrB.ap, ALU.add, (t1.b, nmrB.b), (t1.b,))
            act(t3.ap, t1.ap, AF.Silu, (t1.b, sm.b), (t3.b,), scale=smc(C_LNG + ct), bias=smc(C_LNB + ct))
            tt(ubT.ap[:, ct, :], t3.ap, szb.ap, ALU.mult, (t3.b, szb.b), (ubk[ct],))
        UB = tuple(ubk)

        if dbg and blk == NBLK - 1:
            P.barrier()
            dt_ = wk.get([128, 16, TB], F32, "dbgt")
            cp(dt_.ap, ubT.ap, UB, (dt_.b,))
            dma("sp", dbg_d[:, 1, :, :], dt_.ap, (dt_.b,), (), "dbg")

        if KSTOP < 3:
            continue
        P.barrier()
        wk.reset()
        gAf = wk.get([128, TB], F32, "gAf")
        gBf = wk.get([128, TB], F32, "gBf")
        m1 = wk.get([128, TB], F32, "m1")
        m2 = wk.get([128, TB], F32, "m2")
        mik = [Buf(f"mi{c}") for c in range(KC)]
        for dtile in range(KC):
            cs = slice(dtile * 128, (dtile + 1) * 128)
            s_a = load_w(w_bra_d[:, cs])
            s_ga = load_w(w_in_d[:, OFF_GATE + dtile * 128:OFF_GATE + (dtile + 1) * 128])
            s_b = load_w(w_brb_d[:, cs])
            s_gb = load_w(w_in_d[:, OFF_GATE + D + dtile * 128:OFF_GATE + D + (dtile + 1) * 128])
            b1 = big()
            for kc in range(KC):
                mm(b1.ap[:, 0:TB], wpool.ap[:, s_a, kc, :], oaT.ap[:, kc, :], kc == 0, kc == KC - 1, (oaT.b, wslot[s_a]), (b1.b,))
            b2 = big()
            for kc in range(KC):
                mm(b2.ap[:, 0:TB], wpool.ap[:, s_ga, kc, :], hT.ap[:, kc, :], kc == 0, kc == KC - 1, HT + (wslot[s_ga],), (b2.b,))
            act(gAf.ap, b2.ap[:, 0:TB], AF.Sigmoid, (b2.b, sm.b), (gAf.b,), bias=smc(C_BGATE + dtile))
            tt(m1.ap, b1.ap[:, 0:TB], gAf.ap, ALU.mult, (b1.b, gAf.b), (m1.b,))
            b3 = big()
            for kc in range(KC):
                mm(b3.ap[:, 0:TB], wpool.ap[:, s_b, kc, :], ubT.ap[:, kc, :], kc == 0, kc == KC - 1, UB + (wslot[s_b],), (b3.b,))
            b4 = big()
            for kc in range(KC):
                mm(b4.ap[:, 0:TB], wpool.ap[:, s_gb, kc, :], hT.ap[:, kc, :], kc == 0, kc == KC - 1, HT + (wslot[s_gb],), (b4.b,))
            act(gBf.ap, b4.ap[:, 0:TB], AF.Sigmoid, (b4.b, sm.b), (gBf.b,), bias=smc(C_BGATE + 16 + dtile))
            tt(m2.ap, b3.ap[:, 0:TB], gBf.ap, ALU.mult, (b3.b, gBf.b), (m2.b,))
            tt(minT.ap[:, dtile, :], m1.ap, m2.ap, ALU.add, (m1.b, m2.b), (mik[dtile],))
        MI = tuple(mik)

        if dbg and blk == NBLK - 1:
            P.barrier()
            dt_ = wk.get([128, 16, TB], F32, "dbgt")
            cp(dt_.ap, minT.ap, MI, (dt_.b,))
            dma("sp", dbg_d[:, 2, :, :], dt_.ap, (dt_.b,), (), "dbg")

        if KSTOP < 4:
            continue
        P.barrier()
        wk.reset()
        xt = [wk.get([128, D], F32, f"xtd{i}") for i in range(2)]
        grow = wk.get([128, D], F32, "grow")
        tmpD = wk.get([128, D], F32, "tmpD")
        junkD = wk.get([128, D], BF, "junkD")
        ss1 = wk.get([128, NB], F32, "ss1")
        l1 = wk.get([128, NB], F32, "l1")
        rstd1 = wk.get([128, NB], F32, "rstd1")
        mxk = [Buf(f"mx{n}") for n in range(NB)]
        dma("sp", grow.ap, rows_d[:, 0:D], (), (grow.b,), "grow")
        for cg in range(4):
            s0 = load_wide(w_out_d[:, cg * 512:(cg + 1) * 512])
            for n in range(NB):
                bk = big()
                for kc in range(KC):
                    mm(bk.ap[:, 0:512], minT.ap[:, kc, nsl(n)], wpool.ap[:, s0:s0 + 4, kc, :], kc == 0, kc == KC - 1,
                       MI + tuple(wslot[s0:s0 + 4]), (bk.b,))
                cp(mixed.ap[:, n, cg * 512:(cg + 1) * 512], bk.ap[:, 0:512], (bk.b,), (mxk[n],),
                   eng=("act" if (n + cg) % 2 else "dve"))
        for n in range(NB):
            xs = xt[n % 2]
            dma("sp", xs.ap, x_d[t0 + n * 128:t0 + (n + 1) * 128, :], (), (xs.b,), f"xt{n % 2}")
            act(junkD.ap, mixed.ap[:, n, :], AF.Square, (mxk[n],), (junkD.b, ss1.b), accum_out=ss1.ap[:, n:n + 1])
            act(l1.ap[:, n:n + 1], ss1.ap[:, n:n + 1], AF.Ln, (ss1.b,), (l1.b,), scale=1.0 / D, bias=EPS)
            act(rstd1.ap[:, n:n + 1], l1.ap[:, n:n + 1], AF.Exp, (l1.b,), (rstd1.b,), scale=-0.5)
            stt(tmpD.ap, mixed.ap[:, n, :], rstd1.ap[:, n:n + 1], grow.ap, ALU.mult, ALU.mult, (mxk[n], rstd1.b, grow.b), (tmpD.b,))
            tt(mixed.ap[:, n, :], tmpD.ap, xs.ap, ALU.add, (tmpD.b, xs.b), (mxk[n],))

        if KSTOP < 5:
            continue
        P.barrier()
        wk.reset()
        grow = wk.get([128, D], F32, "growE")
        x1b = wk.get([128, D], BF, "x1b")
        ptile = wk.get([128, PLE], F32, "ptile")
        pbt = wk.get([128, PLE], BF, "pbt")
        pT = wk.get([128, 2, TB], BF, "pT")
        wpp = wk.get([128, 2, D], BF, "wpp")
        sgE = wk.get([128, 512], F32, "sgE")
        tmpE = [wk.get([128, D], F32, f"tmpE{i}") for i in range(2)]
        junkE = x1b
        vbuf = wk.get([128, NB, D], F32, "vbuf")
        ss2 = wk.get([128, NB], F32, "ss2")
        l2 = wk.get([128, NB], F32, "l2")
        rstd2 = wk.get([128, NB], F32, "rstd2")
        x1k = [Buf(f"x1T{n}") for n in range(NB)]
        vbk = [Buf(f"vb{n}") for n in range(NB)]
        dma("sp", grow.ap, rows_d[:, D:2 * D], (), (grow.b,), "grow")
        for j in range(4):
            dma("pool", wpp.ap[:, :, j * 512:(j + 1) * 512],
                w_pp_d[:, j * 512:(j + 1) * 512].rearrange("(kc p) c -> p kc c", p=128), (), (wpp.b,), "wpp")
        for n in range(NB):
            cp(x1b.ap, mixed.ap[:, n, :], (mxk[n],), (x1b.b,), eng="act")
            for g in range(4):
                bk = big()
                for j in range(4):
                    kc = g * 4 + j
                    mm(bk.ap[:, j * 128:(j + 1) * 128], x1b.ap[:, kc * 128:(kc + 1) * 128], identb, True, True, (x1b.b, cb.b), (bk.b,))
                cp(x1T.ap[:, g * 4:g * 4 + 4, nsl(n)], bk.ap.rearrange("p (a b) -> p a b", a=4), (bk.b,), (x1k[n],),
                   eng=("act" if g % 2 else "dve"))
            dma("sp", ptile.ap, p_d[t0 + n * 128:t0 + (n + 1) * 128, :], (), (ptile.b,), "ptile")
            cp(pbt.ap, ptile.ap, (ptile.b,), (pbt.b,))
            bk = small()
            for j in range(2):
                mm(bk.ap[:, j * 128:(j + 1) * 128], pbt.ap[:, j * 128:(j + 1) * 128], identb, True, True, (pbt.b, cb.b), (bk.b,))
            cp(pT.ap[:, :, nsl(n)], bk.ap[:, 0:256].rearrange("p (a b) -> p a b", a=2), (bk.b,), (pT.b,), eng="act")
        for cg in range(4):
            s0 = load_wide(w_pg_d[:, cg * 512:(cg + 1) * 512])
            for n in range(NB):
                bG = big()
                for kc in range(KC):
                    mm(bG.ap[:, 0:512], x1T.ap[:, kc, nsl(n)], wpool.ap[:, s0:s0 + 4, kc, :], kc == 0, kc == KC - 1,
                       (x1k[n],) + tuple(wslot[s0:s0 + 4]), (bG.b,))
                bE = big()
                for kc in range(2):
                    mm(bE.ap[:, 0:512], pT.ap[:, kc, nsl(n)], wpp.ap[:, kc, cg * 512:(cg + 1) * 512], kc == 0, kc == 1,
                       (pT.b, wpp.b), (bE.b,))
                act(sgE.ap, bG.ap[:, 0:512], AF.Sigmoid, (bG.b,), (sgE.b,))
                tt(vbuf.ap[:, n, cg * 512:(cg + 1) * 512], bE.ap[:, 0:512], sgE.ap, ALU.mult, (bE.b, sgE.b), (vbk[n],))
        for n in range(NB):
            tE = tmpE[n % 2]
            act(junkE.ap, vbuf.ap[:, n, :], AF.Square, (vbk[n],), (junkE.b, ss2.b), accum_out=ss2.ap[:, n:n + 1])
            act(l2.ap[:, n:n + 1], ss2.ap[:, n:n + 1], AF.Ln, (ss2.b,), (l2.b,), scale=1.0 / D, bias=EPS)
            act(rstd2.ap[:, n:n + 1], l2.ap[:, n:n + 1], AF.Exp, (l2.b,), (rstd2.b,), scale=-0.5)
            stt(tE.ap, vbuf.ap[:, n, :], rstd2.ap[:, n:n + 1], grow.ap, ALU.mult, ALU.mult, (vbk[n], rstd2.b, grow.b), (tE.b,))
            tt(tE.ap, tE.ap, mixed.ap[:, n, :], ALU.add, (tE.b, mxk[n]), (tE.b,))
            dma("sp", out_d[t0 + n * 128:t0 + (n + 1) * 128, :], tE.ap, (tE.b,), (), f"st{n % 2}")
        P.barrier()

    P.barrier()
    P.finalize()
    esem = {n: es.enter_context(nc.semaphore(f"e_{n}")) for n in Prog.ENGS}
    dsem = {k: es.enter_context(nc.semaphore(f"d_{k}")) for k in P.dcount}
    block = es.enter_context(nc.Block())

    @block.tensor
    def _(t):
        P.replay("pe", t, esem, dsem)

    @block.scalar
    def _(s):
        P.replay("act", s, esem, dsem)

    @block.vector
    def _(v):
        P.replay("dve", v, esem, dsem)

    @block.gpsimd
    def _(g):
        P.replay("pool", g, esem, dsem)

    @block.sync
    def _(sy):
        P.replay("sp", sy, esem, dsem)

    es.close()
    return nc


def host_consts(g_pre, b_gate, w_conv_qkv, a_log, dt_bias, g_dn_out, w_dw, b_dw, ln_g, ln_b, g_post, g_ple):
    sm = np.zeros((128, NS), np.float32)
    sm[:, C_GPRE:C_GPRE + 16] = g_pre[0].reshape(16, 128).T
    sm[:, C_BGATE:C_BGATE + 32] = b_gate[0].reshape(32, 128).T
    sm[:, C_WCONV:C_WCONV + 192] = w_conv_qkv[0].reshape(4, 48, 128).transpose(2, 1, 0).reshape(128, 192)
    sm[:, C_WDW:C_WDW + 496] = w_dw[0].reshape(CK, 16, 128).transpose(2, 1, 0).reshape(128, 496)
    sm[:, C_BDW:C_BDW + 16] = b_dw[0].reshape(16, 128).T
    sm[:, C_LNG:C_LNG + 16] = ln_g[0].reshape(16, 128).T
    sm[:, C_LNB:C_LNB + 16] = ln_b[0].reshape(16, 128).T
    sm[:, C_GDN] = g_dn_out[0]
    sm[:, C_ALOG:C_ALOG + 16] = np.broadcast_to(a_log[0][None, :], (128, 16))
    sm[:, C_DTB:C_DTB + 16] = np.broadcast_to(dt_bias[0][None, :], (128, 16))
    rows = np.zeros((128, 2 * D), np.float32)
    rows[:, 0:D] = np.broadcast_to(g_post[0][None, :], (128, D))
    rows[:, D:] = np.broadcast_to(g_ple[0][None, :], (128, D))
    i = np.arange(128)
    cst = np.zeros((128, NCST, 128), np.float32)
    cst[:, 0] = np.eye(128)
    cst[:, 1] = (i[:, None] <= i[None, :])
    cst[:, 2] = (i[:, None] > i[None, :])
    cst[:, 3] = 1.0
    cst[:, 4] = NEG * (i[:, None] < i[None, :])
    cst[:, 5] = (i[:, None] > i[None, :])
    for sg in range(7):
        bsz = 1 << sg
        same = (i[:, None] // (2 * bsz)) == (i[None, :] // (2 * bsz))
        mE = same & ((i[:, None] % (2 * bsz)) >= bsz) & ((i[None, :] % (2 * bsz)) < bsz)
        cst[:, 6 + sg] = mE
        cst[:, 13 + sg] = mE.T
    return sm, rows, cst.reshape(128, NCST * 128)


_NC_CACHE = {}


def kernel(x, p, g_pre, w_in, b_gate, w_conv_qkv, a_log, dt_bias, g_dn_out, w_dw, b_dw,
           ln_g, ln_b, w_br_a, w_br_b, w_out, g_post, w_ple_gate, w_ple_proj, g_ple):
    x = np.asarray(x, np.float32)
    p = np.asarray(p, np.float32)
    B, S, _ = x.shape
    TB = 512 if S % 512 == 0 else 128
    f = lambda a: np.ascontiguousarray(np.asarray(a, np.float32))
    sm, rows, cst = host_consts(*[np.asarray(a, np.float32) for a in
                                  (g_pre, b_gate, w_conv_qkv, a_log, dt_bias, g_dn_out, w_dw, b_dw, ln_g, ln_b, g_post, g_ple)])
    key = (S, TB)
    if key not in _NC_CACHE:
        _NC_CACHE[key] = build_nc(S, TB)
    nc = _NC_CACHE[key]
    shared = {"w_in": f(w_in[0]), "w_br_a": f(w_br_a[0]), "w_br_b": f(w_br_b[0]), "w_out": f(w_out[0]),
              "w_ple_gate": f(w_ple_gate[0]), "w_ple_proj": f(w_ple_proj[0]), "smalls": sm, "rows": rows, "cst": cst}
    in_maps = []
    for b in range(B):
        m = dict(shared)
        m["x"] = f(x[b])
        m["p"] = f(p[0, b])
        in_maps.append(m)
    res = run_bass_kernel_spmd(nc, in_maps, core_ids=list(range(B)))
    return np.stack([np.asarray(r["out"], np.float32) for r in res.results], axis=0)
```
